# Optimizing a Trainium2 kernel written in Bass

```python
import math
import jax, jax.numpy as jnp
from jax import lax
import numpy as np

D_MODEL = 1024
BATCH = 8
SEQ = 4096
DEPTH = 2

MIX_WIDTH = D_MODEL
A_WIDTH = MIX_WIDTH // 2
A_GROUPS = 4
A_GROUP_DIM = A_WIDTH // A_GROUPS
CHUNK = 128
B_WIDTH = MIX_WIDTH - A_WIDTH
HEAD_DIM = 64
B_HEADS = B_WIDTH // HEAD_DIM
ROT_DIM = HEAD_DIM // 4
ROPE_THETA = 500000.0
DILATED_PATTERNS = ((128, 1), (512, 4), (2048, 16))
IN_COLS = 2 * A_WIDTH + 3 * B_WIDTH
D_FF = 4 * D_MODEL
CONV_WIDTH = 3
EPS = 1e-6
NEG_INF = -1e30

kernel_name = 'hybrid_gmlp_dilated_attn_convffn'


def rmsnorm(x, g):
    xf = x.astype(jnp.float32)
    y = xf * lax.rsqrt(jnp.mean(xf * xf, axis=-1, keepdims=True) + EPS)
    return (y * g.astype(jnp.float32)).astype(x.dtype)


def layernorm(x, g, b):
    xf = x.astype(jnp.float32)
    mu = jnp.mean(xf, axis=-1, keepdims=True)
    xc = xf - mu
    y = xc * lax.rsqrt(jnp.mean(xc * xc, axis=-1, keepdims=True) + EPS)
    return (y * g.astype(jnp.float32) + b.astype(jnp.float32)).astype(x.dtype)


def partial_rope(x):
    s = x.shape[1]
    half = ROT_DIM // 2
    inv = ROPE_THETA ** (-jnp.arange(0, ROT_DIM, 2, dtype=jnp.float32) / ROT_DIM)
    ang = jnp.arange(s, dtype=jnp.float32)[:, None] * inv[None, :]
    cos = jnp.cos(ang)[None, :, None, :]
    sin = jnp.sin(ang)[None, :, None, :]
    xf = x.astype(jnp.float32)
    x1, x2 = xf[..., :half], xf[..., half:ROT_DIM]
    out = jnp.concatenate([x1 * cos - x2 * sin, x2 * cos + x1 * sin, xf[..., ROT_DIM:]], axis=-1)
    return out.astype(x.dtype)


def dilated_branch(q, k, v, window, dilation):
    bsz, s, h, dh = q.shape
    band = window // dilation
    n = s // dilation
    nb = -(-n // band)
    pad = nb * band - n

    def to_blocks(t):
        t = t.reshape(bsz, n, dilation, h, dh).transpose(0, 2, 1, 3, 4)
        t = jnp.pad(t, ((0, 0), (0, 0), (0, pad), (0, 0), (0, 0)))
        return t.reshape(bsz, dilation, nb, band, h, dh)

    def with_prev(t):
        prev = jnp.pad(t[:, :, :-1], ((0, 0), (0, 0), (1, 0), (0, 0), (0, 0), (0, 0)))
        return jnp.concatenate([prev, t], axis=3)

    qb = to_blocks(q)
    kc = with_prev(to_blocks(k))
    vc = with_prev(to_blocks(v))
    scores = jnp.einsum('brnqhd,brnkhd->brnhqk', qb, kc).astype(jnp.float32) * (dh ** -0.5)
    qi = jnp.arange(band)[:, None]
    kj = jnp.arange(2 * band)[None, :]
    dist = qi + band - kj
    blk = jnp.arange(nb)[:, None, None]
    valid = (dist >= 0) & (dist <= band) & (blk * band + kj - band >= 0)
    scores = jnp.where(valid[None, None, :, None], scores, NEG_INF)
    m = jnp.max(scores, axis=-1, keepdims=True)
    p = jnp.exp(scores - m)
    l = jnp.sum(p, axis=-1, keepdims=True)
    o = jnp.einsum('brnhqk,brnkhd->brnqhd', p.astype(v.dtype), vc).astype(jnp.float32)
    l_t = l[..., 0].transpose(0, 1, 2, 4, 3)
    lse = (m[..., 0] + jnp.log(l[..., 0])).transpose(0, 1, 2, 4, 3)
    o = o / l_t[..., None]
    o = o.reshape(bsz, dilation, nb * band, h, dh)[:, :, :n]
    o = o.transpose(0, 2, 1, 3, 4).reshape(bsz, s, h, dh)
    lse = lse.reshape(bsz, dilation, nb * band, h)[:, :, :n]
    lse = lse.transpose(0, 2, 1, 3).reshape(bsz, s, h)
    return o, lse


def mixer_spatial_gating(za, v_norm_g, v_norm_b, w_spatial, b_spatial):
    bsz, s, _ = za.shape
    za = jax.nn.gelu(za, approximate=False)
    u, va = za[..., :A_WIDTH], za[..., A_WIDTH:]
    va = layernorm(va, v_norm_g, v_norm_b)
    vch = va.reshape(bsz, s // CHUNK, CHUNK, A_GROUPS, A_GROUP_DIM)
    ws = w_spatial * jnp.tril(jnp.ones((CHUNK, CHUNK), dtype=w_spatial.dtype))
    sg = jnp.einsum('gpq,bnqgc->bnpgc', ws, vch) + b_spatial.T[None, None, :, :, None]
    return u * sg.reshape(bsz, s, A_WIDTH)


def mixer_dilated_attention(zb):
    bsz, s, _ = zb.shape
    qkv = zb.reshape(bsz, s, 3, B_HEADS, HEAD_DIM)
    q = partial_rope(qkv[:, :, 0])
    k = partial_rope(qkv[:, :, 1])
    v = qkv[:, :, 2]
    outs, lses = zip(*[dilated_branch(q, k, v, w, d) for (w, d) in DILATED_PATTERNS])
    alpha = jax.nn.softmax(jnp.stack(lses, axis=0), axis=0)
    o = jnp.sum(alpha[..., None] * jnp.stack(outs, axis=0), axis=0)
    return o.reshape(bsz, s, B_WIDTH).astype(zb.dtype)


def conv_ffn(h, w_up, conv_w, conv_b, w_down):
    s = h.shape[1]
    up = jnp.einsum('bsd,df->bsf', h, w_up)
    up_pad = jnp.pad(up, ((0, 0), (CONV_WIDTH - 1, 0), (0, 0)))
    conv = conv_b + sum(conv_w[i] * up_pad[:, i:i + s] for i in range(CONV_WIDTH))
    gate, val = conv[..., :D_FF], conv[..., D_FF:]
    y = jax.nn.gelu(gate, approximate=True) * val
    return jnp.einsum('bsf,fd->bsd', y, w_down)


def setup_inputs(seed: int = 0) -> dict:
    key = jax.random.key(seed)
    ks = jax.random.split(key, 18)
    f32 = jnp.float32

    def nrm(k, shape, scale):
        return jax.random.normal(k, shape, f32) * scale

    def gain(k, shape):
        return 1.0 + 0.05 * jax.random.normal(k, shape, f32)

    L = DEPTH
    return {
        'x': jax.random.normal(ks[0], (BATCH, SEQ, D_MODEL), f32),
        'pre_mix_norm': gain(ks[1], (L, D_MODEL)),
        'w_in': nrm(ks[2], (L, D_MODEL, IN_COLS), D_MODEL ** -0.5),
        'v_norm_g': gain(ks[3], (L, A_WIDTH)),
        'v_norm_b': nrm(ks[4], (L, A_WIDTH), 0.02),
        'w_spatial': nrm(ks[5], (L, A_GROUPS, CHUNK, CHUNK), CHUNK ** -0.5),
        'b_spatial': gain(ks[6], (L, A_GROUPS, CHUNK)),
        'out_norm_a': gain(ks[7], (L, A_WIDTH)),
        'out_norm_b': gain(ks[8], (L, B_WIDTH)),
        'w_out': nrm(ks[9], (L, MIX_WIDTH, D_MODEL), MIX_WIDTH ** -0.5),
        'post_mix_norm': gain(ks[10], (L, D_MODEL)),
        'pre_ffn_norm': gain(ks[11], (L, D_MODEL)),
        'w_up': nrm(ks[12], (L, D_MODEL, 2 * D_FF), D_MODEL ** -0.5),
        'conv_w': nrm(ks[13], (L, CONV_WIDTH, 2 * D_FF), CONV_WIDTH ** -0.5),
        'conv_b': nrm(ks[14], (L, 2 * D_FF), 0.02),
        'w_down': nrm(ks[15], (L, D_FF, D_MODEL), D_FF ** -0.5),
        'post_ffn_norm': gain(ks[16], (L, D_MODEL)),
    }


def reference(x, pre_mix_norm, w_in, v_norm_g, v_norm_b, w_spatial, b_spatial,
              out_norm_a, out_norm_b, w_out, post_mix_norm, pre_ffn_norm,
              w_up, conv_w, conv_b, w_down, post_ffn_norm):
    for l in range(DEPTH):
        h = rmsnorm(x, pre_mix_norm[l])
        proj = jnp.einsum('bsd,de->bse', h, w_in[l])
        o_a = mixer_spatial_gating(proj[..., :2 * A_WIDTH], v_norm_g[l], v_norm_b[l],
                                   w_spatial[l], b_spatial[l])
        o_b = mixer_dilated_attention(proj[..., 2 * A_WIDTH:])
        mixed = jnp.concatenate([rmsnorm(o_a, out_norm_a[l]), rmsnorm(o_b, out_norm_b[l])], axis=-1)
        y = jnp.einsum('bse,ed->bsd', mixed, w_out[l])
        x = x + rmsnorm(y, post_mix_norm[l])
        h = rmsnorm(x, pre_ffn_norm[l])
        f = conv_ffn(h, w_up[l], conv_w[l], conv_b[l], w_down[l])
        x = x + rmsnorm(f, post_ffn_norm[l])
    return x
```

```python
import numpy as np
import concourse.bass as bass
import concourse.mybir as mybir
from contextlib import ExitStack

F32 = mybir.dt.float32
BF16 = mybir.dt.bfloat16
AF = mybir.ActivationFunctionType
ALU = mybir.AluOpType
AX = mybir.AxisListType


class Eng:
    def __init__(self, fw, name, e, sem):
        self.fw = fw
        self.name = name
        self.e = e
        self.sem = sem
        self.count = 0
        self.waited = {}
        self.pending = False
        self.prog = []
        self.cur_waits = []

    def wait(self, sem, val):
        if sem is self.sem and self.name == "pe":
            return
        key = id(sem)
        if self.waited.get(key, 0) >= val:
            return
        self.waited[key] = val
        self.cur_waits.append((sem, val))

    def push(self, fn, incs):
        self.prog.append((self.cur_waits, fn, incs))
        self.cur_waits = []

    def replay(self, e):
        for waits, fn, incs in self.prog:
            for sem, val in waits:
                e.wait_ge(sem, val)
            if fn is None:
                continue
            ins = fn(e)
            for sem, n in incs:
                ins.then_inc(sem, n)


class Buf:
    def __init__(self, fw, name, ap=None, is_dram=False):
        self.fw = fw
        self.name = name
        self.ap = ap
        self.is_dram = is_dram
        self.writers = []
        self.readers = []
        self.dsem = None
        self.dcount = 0
        fw.all_bufs.append(self)

    def __getitem__(self, idx):
        return self.ap[idx]

    def dma_sem(self):
        if self.dsem is None:
            self.dsem, self.dcount = self.fw.take_dsem(self.name)
            self.fw.dbufs.append(self)
        return self.dsem


class FW:
    def __init__(self, nc, stack):
        self.nc = nc
        self.stack = stack
        self.sem_stack = stack
        self.uid = 0
        self.nsem = 0
        self.engs = {}
        self.all_bufs = []
        for name, e in (("pe", nc.tensor), ("act", nc.scalar), ("dve", nc.vector),
                        ("pool", nc.gpsimd), ("sp", nc.sync)):
            self.engs[name] = Eng(self, name, e, self.new_sem("e_" + name))
        self.pe = self.engs["pe"]; self.act = self.engs["act"]; self.dve = self.engs["dve"]
        self.pool = self.engs["pool"]; self.sp = self.engs["sp"]
        self.capture = None
        self.arena = None
        self.cur_atom = []
        self.sem_pool = []
        self.dbufs = []

    def take_dsem(self, name):
        if self.sem_pool:
            return self.sem_pool.pop()
        return self.new_sem("d_" + name), 0

    def barrier(self):
        engs = list(self.engs.values())
        for e in engs:
            for f in engs:
                if f is not e and f.count > 0:
                    e.wait(f.sem, f.count)
            for b in self.dbufs:
                if b.dcount > 0:
                    e.wait(b.dsem, b.dcount)
            e.push(None, [])
        for b in self.dbufs:
            self.sem_pool.append((b.dsem, b.dcount))
            b.dsem = None
            b.dcount = 0
        self.dbufs = []
        for b in self.all_bufs:
            b.writers = []
            b.readers = []

    def new_sem(self, name):
        self.nsem += 1
        name = "%s_s%d" % (name, self.nsem)
        return self.sem_stack.enter_context(self.nc.semaphore(name))

    def init_arena(self):
        nbytes = (self.nc.sbuf_bytes_remaining // 64) * 64 - 128
        self.arena = self.sem_stack.enter_context(self.nc.sbuf_tensor("arena", [128, nbytes], mybir.dt.uint8))
        self.sb_off = 0
        self.sb_cap = nbytes
        self.parena = self.sem_stack.enter_context(self.nc.psum_tensor("parena", [128, 8, 512], F32))
        self.ps_off = 0

    @staticmethod
    def _reshape(ap, shape):
        if len(shape) == 2:
            return ap
        if len(shape) == 3:
            return ap.rearrange("p (a b) -> p a b", a=shape[1])
        if len(shape) == 4:
            return ap.rearrange("p (a b c) -> p a b c", a=shape[1], b=shape[2])
        raise ValueError(shape)

    def _sb_release(self, off):
        self.sb_off = off

    def _ps_release(self, off):
        self.ps_off = off

    def sbuf(self, name, shape, dtype):
        if self.arena is None:
            self.init_arena()
        assert shape[0] == 128
        esz = 4 if dtype == F32 else 2
        n = 1
        for d in shape[1:]:
            n *= d
        nb = n * esz
        nb_al = (nb + 63) // 64 * 64
        off = self.sb_off
        assert off + nb_al <= self.sb_cap, ("SBUF arena overflow", name, off, nb_al, self.sb_cap)
        self.sb_off = off + nb_al
        self.stack.callback(self._sb_release, off)
        ap = self.arena[:, off:off + nb].bitcast(dtype)
        return Buf(self, name, self._reshape(ap, shape))

    def psum(self, name, shape, dtype=F32):
        if self.arena is None:
            self.init_arena()
        assert shape[0] == 128
        esz = 4 if dtype == F32 else 2
        n = 1
        for d in shape[1:]:
            n *= d
        nb = n * esz
        banks = (nb + 2047) // 2048
        b0 = self.ps_off
        assert b0 + banks <= 8, ("PSUM arena overflow", name)
        self.ps_off = b0 + banks
        self.stack.callback(self._ps_release, b0)
        if banks == 1:
            ap = self.parena[:, b0, :]
        else:
            ap = self.parena[:, b0:b0 + banks, :].rearrange("p b c -> p (b c)")
        if dtype != F32:
            ap = ap.bitcast(dtype)
        ap = ap[:, 0:n]
        return Buf(self, name, self._reshape(ap, shape))

    def dram(self, name, ap=None):
        return Buf(self, name, ap, is_dram=True)

    def view(self, name, ap):
        return Buf(self, name, ap)

    def _wait_set(self, eng, tickets):
        best = {}
        for sem, val in tickets:
            k = id(sem)
            if k not in best or best[k][1] < val:
                best[k] = (sem, val)
        for sem, val in best.values():
            eng.wait(sem, val)

    def _deps(self, eng, reads, writes):
        t = []
        for b in reads:
            t += b.writers
        for b in writes:
            t += b.writers
            t += b.readers
        self._wait_set(eng, t)

    def _record(self, ticket, reads, writes, append_write=False):
        for b in reads:
            b.readers.append(ticket)
            if len(b.readers) > 12:
                b.readers = _compact(b.readers)
        for b in writes:
            if append_write or b.is_dram:
                b.writers.append(ticket)
                if len(b.writers) > 12:
                    b.writers = _compact(b.writers)
            else:
                b.writers = [ticket]
                b.readers = []

    def op(self, eng, fn, reads=(), writes=(), inc=True):
        if self.capture is not None:
            self.cur_atom.append(("op", (eng, fn, reads, writes, inc)))
            if inc:
                self.capture.append(self.cur_atom)
                self.cur_atom = []
            return
        self._deps(eng, reads, writes)
        ticket = (eng.sem, eng.count + 1)
        if inc:
            eng.push(fn, [(eng.sem, 1)])
            eng.count += 1
            eng.pending = False
        else:
            eng.push(fn, [])
            eng.pending = True
        self._record(ticket, reads, writes)

    def dma(self, eng, out, in_, src, dst, owner=None, same_fill=False, **kw):
        if owner is None:
            owner = dst if not dst.is_dram else src
        if self.capture is not None:
            self.cur_atom.append(("dma", (eng, out, in_, src, dst, owner, same_fill, kw)))
            self.capture.append(self.cur_atom)
            self.cur_atom = []
            return
        t = []
        if not src.is_dram:
            t += src.writers
        if not dst.is_dram:
            if not same_fill:
                t += dst.writers
            t += dst.readers
        self._wait_set(eng, t)
        sem = owner.dma_sem()
        eng.push(lambda e: e.dma_start(out=out, in_=in_, **kw), [(sem, 16)])
        owner.dcount += 16
        ticket = (sem, owner.dcount)
        self._record(ticket, [] if src.is_dram else [src], [] if dst.is_dram else [dst], append_write=same_fill)

    def begin_chain(self):
        assert self.capture is None
        self.capture = []
        self.cur_atom = []

    def end_chain(self):
        assert not self.cur_atom
        c = self.capture
        self.capture = None
        return c

    def interleave(self, *chains):
        chains = [list(c) for c in chains]
        while any(chains):
            for c in chains:
                if c:
                    for kind, a in c.pop(0):
                        if kind == "op":
                            self.op(a[0], a[1], reads=a[2], writes=a[3], inc=a[4])
                        else:
                            self.dma(a[0], a[1], a[2], a[3], a[4], owner=a[5], same_fill=a[6], **a[7])

    def wait_all(self, eng, bufs):
        for b in bufs:
            for w in b.writers:
                eng.wait(*w)
            for r in b.readers:
                eng.wait(*r)
        eng.push(None, [])

    def emit(self, block):
        block.gpsimd(lambda e: self.pool.replay(e))
        block.sync(lambda e: self.sp.replay(e))
        block.tensor(lambda e: self.pe.replay(e))
        block.vector(lambda e: self.dve.replay(e))
        block.scalar(lambda e: self.act.replay(e))

    def dram_reset(self, b):
        b.writers = []
        b.readers = []


def _compact(readers):
    best = {}
    for sem, val in readers:
        k = id(sem)
        if k not in best or best[k][1] < val:
            best[k] = (sem, val)
    return list(best.values())


from concourse.bass_utils import run_bass_kernel_spmd

S = 4096
DM = 1024
NT = S // 128
EPS = 1e-6
PATTERNS = (1, 4, 16)
NCORES = 8
CUT = 99
FSUB = 9
M2S = 9
M2N = 10**9
SUB = 2
NTL = NT


class _FullAP:
    def __init__(self, ap, full):
        self._ap = ap
        self.tensor_full = full

    def __getitem__(self, idx):
        return self._ap[idx]


def build_program(n_layers=2, stop_after=None, debug=False):
    nc = bass.Bass("TRN2", target_bir_lowering=False)
    IN = lambda name, shape, dt=F32: nc.dram_tensor(name, list(shape), dt, kind="ExternalInput").ap()
    dbgkind = "ExternalOutput" if debug else "Internal"
    SCR = lambda name, shape, dt: nc.dram_tensor(name, list(shape), dt, kind=dbgkind).ap()

    x_in = IN("x", [S, DM])
    w_in = IN("w_in", [2, 128, 8, 2560])
    w_out = IN("w_out", [2, 128, 8, 1024])
    w_up = IN("w_up", [2, 32, 128, 8, 256])
    w_down = IN("w_down", [2, 32, 128, 1024])
    wsT_in = IN("wsT", [2, 128, 4, 128])
    bsp_in = IN("bsp", [2, 128, 4])
    vecA_in = IN("vecA", [2, 128, 2560])
    vecB_in = IN("vecB", [2, 128, 2560])
    vecC_in = IN("vecC", [2, 128, 1024])
    cpk_in = IN("cpk", [2, 128, 64, 4])
    rope_in = IN("rope", [128, 2, NT, 8])
    cst_in = IN("cst", [128, 768])
    out = nc.dram_tensor("out", [S, DM], F32, kind="ExternalOutput").ap()

    Vs = SCR("Vs", [S, 520], BF16)
    MA = SCR("MA", [S, 512], BF16)
    Od = [SCR("O%d" % d, [S, 520], F32) for d in PATTERNS]
    X1 = SCR("X1", [S, DM], F32)
    XA = SCR("XA", [S, DM], F32)
    WUb = nc.dram_tensor("WUb", [32, 128, 8, 256], BF16).ap()
    WDb = nc.dram_tensor("WDb", [32, 128, 1024], BF16).ap()

    with ExitStack() as st:
        P = FW(nc, st)
        pe, act, dve, pool, sp = P.pe, P.act, P.dve, P.pool, P.sp
        DR = P.dram("dram")
        cvt = [Buf(P, "cvt%d" % i) for i in range(4)]

        ident = P.sbuf("ident", [128, 128], BF16)
        mask4 = P.sbuf("mask4", [128, 512], BF16)
        triu = P.sbuf("triu", [128, 128], F32)
        rope = P.sbuf("rope", [128, 2, NT, 8], F32)
        mhalf = P.sbuf("mhalf", [128, 1], F32)
        junk = P.sbuf("junk", [128, 1024], BF16)
        NSTAT = 16
        stats = [P.sbuf("stat%d" % i, [128, 16], F32) for i in range(NSTAT)]
        stat_i = [0]

        def new_stat():
            b = stats[stat_i[0] % NSTAT]
            stat_i[0] += 1
            return b

        P.dma(pool, ident[:], cst_in[:, 0:128], DR, ident)
        P.dma(pool, mask4[:], cst_in[:, 256:768], DR, mask4)
        P.dma(sp, triu[:], cst_in[:, 128:256], DR, triu)
        P.dma(sp, rope[:], rope_in, DR, rope)
        P.op(dve, lambda e: e.memset(mhalf[:], -0.5), writes=[mhalf])

        def rstd_of(sb, col, n):
            P.op(dve, lambda e: e.tensor_scalar(sb[:, col + 1:col + 2], sb[:, col:col + 1], 1.0 / n, EPS, ALU.mult, ALU.add),
                 reads=[sb], writes=[sb])
            P.op(pool, lambda e: e.tensor_tensor(sb[:, col + 2:col + 3], sb[:, col + 1:col + 2], mhalf[:], ALU.pow),
                 reads=[sb, mhalf], writes=[sb])
            return sb[:, col + 2:col + 3]

        def sumsq(src_buf, src_ap, sb, col, width):
            P.op(act, lambda e: e.activation(junk[:, 0:width], src_ap, AF.Square, accum_out=sb[:, col:col + 1]),
                 reads=[src_buf], writes=[sb, junk])

        def transposes(src_buf, src_ap_fn, n, pt_buf):
            for c in range(n):
                P.op(pe, lambda e, c=c: e.transpose(pt_buf[:, c, :], src_ap_fn(c), ident[:]),
                     reads=[src_buf, ident], writes=[pt_buf], inc=(c == n - 1))

        for l in range(n_layers):
            Xsrc = x_in if l == 0 else XA
            Xdst = out if l == n_layers - 1 else XA
            with ExitStack() as sA:
                P.stack = sA
                QT = P.sbuf("QT", [128, 4, S], BF16)
                KT = P.sbuf("KT", [128, 4, S], BF16)
                with ExitStack() as s1:
                    P.stack = s1
                    win = P.sbuf("win", [128, 8, 2560], BF16)
                    vecA = P.sbuf("vecA", [128, 2560], F32)
                    wsT = P.sbuf("wsT", [128, 4, 128], BF16)
                    wsf = P.sbuf("wsf", [128, 4, 128], F32)
                    bsp = P.sbuf("bsp", [128, 4], F32)
                    xr = [P.sbuf("xr%d" % i, [128, DM], F32) for i in range(3)]
                    hb = P.sbuf("hb", [128, DM], BF16)
                    hT = [P.sbuf("hT%d" % i, [128, 8, 128], BF16) for i in range(2)]
                    gu = P.sbuf("gu", [128, 512], F32)
                    gv = P.sbuf("gv", [128, 512], F32)
                    vn = P.sbuf("vn", [128, 512], F32)
                    vn3 = P.sbuf("vn3", [128, 512], BF16)
                    oa = P.sbuf("oa", [128, 512], F32)
                    mar = [P.sbuf("mar%d" % i, [128, 512], BF16) for i in range(2)]
                    qk = P.sbuf("qk", [128, 16, 64], BF16)
                    tA = P.sbuf("tA", [128, 16, 8], F32)
                    tB = P.sbuf("tB", [128, 16, 8], F32)
                    vtr = [P.sbuf("vtr%d" % i, [128, 8, 65], BF16) for i in range(2)]
                    bnst = P.sbuf("bnst", [128, 8], F32)
                    pT = P.psum("pT", [128, 8, 128], BF16)
                    pA = P.psum("pA", [128, 1024], F32)
                    pB = P.psum("pB", [128, 1024], F32)
                    pC = P.psum("pC", [128, 512], F32)
                    psg = P.psum("psg", [128, 512], F32)
                    pqt = P.psum("pqt", [128, 8, 128], BF16)
                    pAu = P.view("pAu", pA[:, 0:512]); pAv = P.view("pAv", pA[:, 512:1024])

                    for c4 in range(4):
                        P.dma(pool, win[:, 2 * c4:2 * c4 + 2, :], w_in[l, :, 2 * c4:2 * c4 + 2, :], DR, win, same_fill=True)
                    P.dma(sp, vecA[:], vecA_in[l], DR, vecA)
                    P.dma(sp, wsf[:], wsT_in[l], DR, wsf)
                    P.dma(sp, bsp[:], bsp_in[l], DR, bsp)
                    for g in range(4):
                        P.op(dve, lambda e, g=g: e.tensor_tensor(wsT[:, g, :], wsf[:, g, :], triu[:], ALU.mult),
                             reads=[wsf, triu], writes=[wsT])
                    for v in vtr:
                        P.op(dve, lambda e, v=v: e.memset(v[:, :, 64:65], 1.0), writes=[v])
                    g_pm = vecA[:, 0:1024]; g_vg = vecA[:, 1024:1536]; g_vb = vecA[:, 1536:2048]; g_oa = vecA[:, 2048:2560]

                    def m1_load(i):
                        xt = xr[i % 3]
                        rows = slice(i * 128, (i + 1) * 128)
                        P.dma(sp, xt[:], Xsrc[rows, :], DR, xt)
                        P.dma(pool, WUb[i], w_up[l, i], DR, DR, owner=cvt[i % 4])

                    def m1_head(i):
                        xt = xr[i % 3]; hTi = hT[i % 2]
                        sb = new_stat()
                        sumsq(xt, xt[:], sb, 0, 1024)
                        r0 = rstd_of(sb, 0, 1024)
                        P.op(dve, lambda e, xt=xt, r0=r0: e.scalar_tensor_tensor(hb[:], xt[:], r0, g_pm, ALU.mult, ALU.mult),
                             reads=[xt, sb, vecA], writes=[hb])
                        transposes(hb, lambda c: hb[:, c * 128:(c + 1) * 128], 8, pT)
                        P.op(act, lambda e, hTi=hTi: e.copy(hTi[:], pT[:]), reads=[pT], writes=[hTi])

                    def m1_proj(i):
                        hTi = hT[i % 2]
                        for ct, (pbuf, pap) in enumerate(((pAu, pA[:, 0:512]), (pAv, pA[:, 512:1024]), (pB, pB[:, 0:512]),
                                                          (pB, pB[:, 512:1024]), (pC, pC[:]))):
                            for c in range(8):
                                P.op(pe, lambda e, c=c, ct=ct, pap=pap, hTi=hTi: e.matmul(
                                    pap, hTi[:, c, :], win[:, c, ct * 512:(ct + 1) * 512], start=(c == 0), stop=(c == 7)),
                                    reads=[hTi, win], writes=[pbuf], inc=(c == 7))

                    def m1_gelu(i):
                        P.op(act, lambda e: e.activation(gu[:], pA[:, 0:512], AF.Gelu), reads=[pAu], writes=[gu])
                        P.op(act, lambda e: e.activation(gv[:], pA[:, 512:1024], AF.Gelu), reads=[pAv], writes=[gv])

                    def m1_rope(i):
                        vt = vtr[i % 2]
                        rows = slice(i * 128, (i + 1) * 128)
                        pBv = pB[:].rearrange("p (h d) -> p h d", d=64)
                        cosb = rope[:, 0, i, :].unsqueeze(1).to_broadcast([128, 16, 8])
                        sinb = rope[:, 1, i, :].unsqueeze(1).to_broadcast([128, 16, 8])
                        x1 = pBv[:, :, 0:8]; x2 = pBv[:, :, 8:16]
                        P.op(dve, lambda e, x1=x1, cosb=cosb: e.tensor_tensor(tA[:], x1, cosb, ALU.mult), reads=[pB, rope], writes=[tA])
                        P.op(dve, lambda e, x2=x2, sinb=sinb: e.tensor_tensor(tB[:], x2, sinb, ALU.mult), reads=[pB, rope], writes=[tB])
                        P.op(dve, lambda e: e.tensor_tensor(qk[:, :, 0:8], tA[:], tB[:], ALU.subtract), reads=[tA, tB], writes=[qk])
                        P.op(dve, lambda e, x2=x2, cosb=cosb: e.tensor_tensor(tA[:], x2, cosb, ALU.mult), reads=[pB, rope], writes=[tA])
                        P.op(dve, lambda e, x1=x1, sinb=sinb: e.tensor_tensor(tB[:], x1, sinb, ALU.mult), reads=[pB, rope], writes=[tB])
                        P.op(dve, lambda e: e.tensor_tensor(qk[:, :, 8:16], tA[:], tB[:], ALU.add), reads=[tA, tB], writes=[qk])
                        P.op(act, lambda e, pBv=pBv: e.copy(qk[:, :, 16:64], pBv[:, :, 16:64]), reads=[pB], writes=[qk])
                        P.op(act, lambda e, vt=vt: e.copy(vt[:, :, 0:64], pC[:].rearrange("p (h d) -> p h d", d=64)),
                             reads=[pC], writes=[vt])
                        P.dma(sp, Vs[rows, :], vt[:].rearrange("p h d -> p (h d)"), vt, DR)

                    def m1_ln(i):
                        sb = new_stat()
                        P.op(dve, lambda e: e.bn_stats(bnst[:, 0:6], gv[:]), reads=[gv], writes=[bnst])
                        P.op(dve, lambda e, sb=sb: e.bn_aggr(sb[:, 4:6], bnst[:, 0:6]), reads=[bnst], writes=[sb])
                        rv = rstd_of(sb, 5, 1.0)
                        P.op(dve, lambda e, sb=sb, rv=rv: e.tensor_scalar(vn[:], gv[:], sb[:, 4:5], rv, ALU.subtract, ALU.mult),
                             reads=[gv, sb], writes=[vn])
                        P.op(dve, lambda e: e.tensor_tensor(vn[:], vn[:], g_vg, ALU.mult), reads=[vn, vecA], writes=[vn])
                        P.op(dve, lambda e: e.tensor_tensor(vn3[:], vn[:], g_vb, ALU.add), reads=[vn, vecA], writes=[vn3])
                        for g in range(4):
                            P.op(pe, lambda e, g=g: e.matmul(psg[:, g * 128:(g + 1) * 128], wsT[:, g, :], vn3[:, g * 128:(g + 1) * 128],
                                                             start=True, stop=True),
                                 reads=[wsT, vn3], writes=[psg], inc=(g == 3))

                    def m1_oa(i):
                        ma = mar[i % 2]
                        rows = slice(i * 128, (i + 1) * 128)
                        for g in range(4):
                            P.op(dve, lambda e, g=g: e.scalar_tensor_tensor(oa[:, g * 128:(g + 1) * 128], psg[:, g * 128:(g + 1) * 128],
                                                                             bsp[:, g:g + 1], gu[:, g * 128:(g + 1) * 128], ALU.add, ALU.mult),
                                 reads=[psg, bsp, gu], writes=[oa])
                        sb = new_stat()
                        sumsq(oa, oa[:], sb, 0, 512)
                        ra = rstd_of(sb, 0, 512)
                        P.op(dve, lambda e, ma=ma, ra=ra: e.scalar_tensor_tensor(ma[:], oa[:], ra, g_oa, ALU.mult, ALU.mult),
                             reads=[oa, sb, vecA], writes=[ma])
                        P.dma(sp, MA[rows, :], ma[:], ma, DR)

                    def m1_qkT(i):
                        rows = slice(i * 128, (i + 1) * 128)
                        qkf = qk[:].rearrange("p h d -> p (h d)")
                        transposes(qk, lambda c, qkf=qkf: qkf[:, c * 128:(c + 1) * 128], 8, pqt)
                        P.op(act, lambda e, rows=rows: e.copy(QT[:, :, rows], pqt[:, 0:4, :]), reads=[pqt], writes=[QT])
                        P.op(act, lambda e, rows=rows: e.copy(KT[:, :, rows], pqt[:, 4:8, :]), reads=[pqt], writes=[KT])

                    m1_load(0); m1_load(1)
                    m1_head(0)
                    for i in range(NT):
                        if i + 2 < NT:
                            m1_load(i + 2)
                        m1_proj(i)
                        P.begin_chain(); m1_gelu(i); m1_ln(i); m1_oa(i); cA = P.end_chain()
                        P.begin_chain(); m1_rope(i); m1_qkT(i); cC = P.end_chain()
                        cB = []
                        if i + 1 < NT:
                            P.begin_chain(); m1_head(i + 1); cB = P.end_chain()
                        P.interleave(cA, cC, cB)
                P.barrier()
                if stop_after == "M1":
                    break
                with ExitStack() as s2:
                    P.stack = s2
                    NVA = 3
                    var_ = [P.sbuf("va%d" % i, [128, 8, 65], BF16) for i in range(NVA)]
                    NPT = 3
                    praw = [P.sbuf("praw%d" % i, [128, 1024], BF16) for i in range(NPT)]
                    ptb = [P.sbuf("pt%d" % i, [128, 1024], BF16) for i in range(NPT)]
                    obr = [P.sbuf("ob%d" % i, [128, 2, 260], F32) for i in range(2)]
                    Sb = [P.psum("S%d" % i, [128, 2, 512], F32) for i in range(2)]
                    pO = [P.psum("pO%d" % i, [128, 2, 512], F32) for i in range(2)]
                    units = []
                    vi = 0
                    bi = 0
                    for pi, d in enumerate(PATTERNS):
                        for r in range(d):
                            prev_slot = None
                            for n in range(S // (128 * d)):
                                start = r + d * 128 * n
                                sl = slice(start, start + d * 127 + 1, d)
                                psl = slice(start - d * 128, start - d + 1, d)
                                va = var_[vi % NVA]; vi += 1
                                units.append(("load", va, sl))
                                for hpp in range(2):
                                    units.append(("unit", pi, sl, psl, n, hpp, va, prev_slot, bi))
                                prev_slot = va
                                bi += 1
                    v3d = lambda t: t[:].rearrange("p (a c) -> p a c", a=2)
                    v4c = lambda t3: t3.rearrange("p a (q w c) -> p a q w c", q=2, w=2)[:, :, :, 0, :]
                    mask3 = mask4[:].unsqueeze(1).to_broadcast([128, 2, 512])
                    mask4c = mask4[:].rearrange("p (q w c) -> p q w c", q=2, w=2)[:, :, 0, :].unsqueeze(1).to_broadcast([128, 2, 2, 128])

                    def emit_qk(u, k):
                        _, pi, sl, psl, n, hpp, va, vprev, b = u
                        Sk = Sb[k % 2]
                        nw = 2 if n > 0 else 1
                        nmm = 4 * nw
                        j = 0
                        for q in range(2):
                            hp = 2 * hpp + q
                            for hl in range(2):
                                for w in range(nw):
                                    ksl = sl if w == 0 else psl
                                    c0 = (q * 2 + w) * 128
                                    j += 1
                                    P.op(pe, lambda e, Sk=Sk, c0=c0, hl=hl, ksl=ksl, hp=hp: e.matmul(
                                        Sk[:, hl, c0:c0 + 128], KT[hl * 64:(hl + 1) * 64, hp, ksl], QT[hl * 64:(hl + 1) * 64, hp, sl],
                                        start=True, stop=True), reads=[KT, QT], writes=[Sk], inc=(j == nmm))

                    def emit_exp(u, k):
                        _, pi, sl, psl, n, hpp, va, vprev, b = u
                        Sk = Sb[k % 2]; pr = praw[k % NPT]; pt = ptb[k % NPT]
                        if n > 0:
                            P.op(act, lambda e: e.activation(v3d(pr), Sk[:], AF.Exp, scale=0.125), reads=[Sk], writes=[pr])
                            P.op(dve, lambda e: e.tensor_tensor(v3d(pt), v3d(pr), mask3, ALU.mult), reads=[pr, mask4], writes=[pt])
                        else:
                            P.op(act, lambda e: e.activation(v4c(v3d(pr)), v4c(Sk[:]), AF.Exp, scale=0.125), reads=[Sk], writes=[pr])
                            P.op(dve, lambda e: e.tensor_tensor(v4c(v3d(pt)), v4c(v3d(pr)), mask4c, ALU.mult), reads=[pr, mask4], writes=[pt])

                    def emit_pv(u, k):
                        _, pi, sl, psl, n, hpp, va, vprev, b = u
                        pt = ptb[k % NPT]; po = pO[b % 2]
                        pt3 = v3d(pt)
                        for q in range(2):
                            for hl in range(2):
                                h = (2 * hpp + q) * 2 + hl
                                oap = po[:, h // 4, (h % 4) * 65:(h % 4) * 65 + 65]
                                last = (q == 1 and hl == 1)
                                c0 = (q * 2) * 128
                                P.op(pe, lambda e, oap=oap, hl=hl, h=h, c0=c0: e.matmul(oap, pt3[:, hl, c0:c0 + 128], va[:, h, :],
                                                                                       start=True, stop=(n == 0)),
                                     reads=[pt, va], writes=[po], inc=(last and n == 0))
                                if n > 0:
                                    P.op(pe, lambda e, oap=oap, hl=hl, h=h, c0=c0: e.matmul(oap, pt3[:, hl, c0 + 128:c0 + 256],
                                                                                           vprev[:, h, :], start=False, stop=True),
                                         reads=[pt, vprev], writes=[po], inc=last)
                        if hpp == 1:
                            ob = obr[b % 2]
                            P.op(dve, lambda e: e.tensor_copy(ob[:], po[:, :, 0:260]), reads=[po], writes=[ob])
                            P.dma(sp, Od[pi][sl, :], ob[:].rearrange("p a c -> p (a c)"), ob, DR)

                    pend = None
                    k = 0
                    for u in units:
                        if k >= M2N:
                            break
                        if u[0] == "load":
                            P.dma(sp, u[1][:].rearrange("p h d -> p (h d)"), Vs[u[2], :], DR, u[1])
                            continue
                        if M2S >= 1:
                            emit_qk(u, k)
                        if M2S >= 2:
                            emit_exp(u, k)
                        if pend is not None and M2S >= 4:
                            emit_pv(*pend)
                        pend = (u, k)
                        k += 1
                    if M2S >= 4:
                        emit_pv(*pend)
                P.barrier()
            P.stack = st
            if stop_after in ("M1", "M2"):
                break
            with ExitStack() as sB:
                P.stack = sB
                H2T = P.sbuf("H2T", [128, 8, S], BF16)
                with ExitStack() as s3:
                    P.stack = s3
                    wout = P.sbuf("wout", [128, 8, 1024], BF16)
                    vecB = P.sbuf("vecB", [128, 2560], F32)
                    g_ob = vecB[:, 0:512]; g_post = vecB[:, 512:1536]; g_pre = vecB[:, 1536:2560]
                    otr = [[P.sbuf("ot%d_%d" % (i, j), [128, 8, 65], F32) for j in range(3)] for i in range(3)]
                    mx = [P.sbuf("mx%d" % i, [128, 1024], BF16) for i in range(3)]
                    xr = [P.sbuf("xr%d" % i, [128, DM], F32) for i in range(3)]
                    osum = P.sbuf("osum", [128, 8, 65], F32)
                    rl = P.sbuf("rl", [128, 8], F32)
                    obf = P.sbuf("obf", [128, 512], F32)
                    mTr = [P.sbuf("mT%d" % i, [128, 8, 128], BF16) for i in range(2)]
                    tt = P.sbuf("tt", [128, DM], F32)
                    x1r = [P.sbuf("x1r%d" % i, [128, DM], F32) for i in range(2)]
                    h2 = P.sbuf("h2", [128, DM], BF16)
                    pT = P.psum("pT3", [128, 8, 128], BF16)
                    pyr = [P.psum("py%d" % i, [128, 1024], F32) for i in range(2)]
                    pT2 = P.psum("pT4", [128, 8, 128], BF16)
                    P.dma(pool, wout[:], w_out[l], DR, wout)
                    P.dma(sp, vecB[:], vecB_in[l], DR, vecB)
                    def m3_load(i):
                        rows = slice(i * 128, (i + 1) * 128)
                        ot = otr[i % 3]; m = mx[i % 3]; xt = xr[i % 3]
                        for j in range(3):
                            P.dma(sp, ot[j][:].rearrange("p h d -> p (h d)"), Od[j][rows, :], DR, ot[j])
                        P.dma(sp, m[:, 0:512], MA[rows, :], DR, m)
                        P.dma(sp, xt[:], Xsrc[rows, :], DR, xt)
                        P.dma(pool, WDb[i], w_down[l, i], DR, DR, owner=cvt[i % 4])

                    def m3_comb(i):
                        ot = otr[i % 3]; m = mx[i % 3]; xt = xr[i % 3]
                        P.op(dve, lambda e, ot=ot: e.tensor_tensor(osum[:], ot[0][:], ot[1][:], ALU.add), reads=[ot[0], ot[1]], writes=[osum])
                        P.op(dve, lambda e, ot=ot: e.tensor_tensor(osum[:], osum[:], ot[2][:], ALU.add), reads=[osum, ot[2]], writes=[osum])
                        P.op(dve, lambda e: e.reciprocal(rl[:], osum[:, :, 64]), reads=[osum], writes=[rl])
                        P.op(dve, lambda e: e.tensor_tensor(obf[:].rearrange("p (h d) -> p h d", d=64), osum[:, :, 0:64],
                                                            rl[:].unsqueeze(2).to_broadcast([128, 8, 64]), ALU.mult),
                             reads=[osum, rl], writes=[obf])
                        sb = new_stat()
                        sumsq(obf, obf[:], sb, 0, 512)
                        rb = rstd_of(sb, 0, 512)
                        P.op(dve, lambda e, m=m, rb=rb: e.scalar_tensor_tensor(m[:, 512:1024], obf[:], rb, g_ob, ALU.mult, ALU.mult),
                             reads=[obf, sb, vecB], writes=[m])
                        transposes(m, lambda c, m=m: m[:, c * 128:(c + 1) * 128], 8, pT)
                        mT = mTr[i % 2]
                        P.op(act, lambda e, mT=mT: e.copy(mT[:], pT[:]), reads=[pT], writes=[mT])

                    def m3_y(i):
                        mT = mTr[i % 2]; py = pyr[i % 2]
                        for half in range(2):
                            for c in range(8):
                                P.op(pe, lambda e, c=c, half=half, py=py, mT=mT: e.matmul(py[:, half * 512:(half + 1) * 512], mT[:, c, :],
                                                                            wout[:, c, half * 512:(half + 1) * 512],
                                                                            start=(c == 0), stop=(c == 7)),
                                     reads=[mT, wout], writes=[py], inc=(c == 7 and half == 1))

                    def m3_n1(i):
                        py = pyr[i % 2]
                        sb = new_stat()
                        sumsq(py, py[:], sb, 0, 1024)
                        ry = rstd_of(sb, 0, 1024)
                        P.op(dve, lambda e, ry=ry, py=py: e.scalar_tensor_tensor(tt[:], py[:], ry, g_post, ALU.mult, ALU.mult),
                             reads=[py, sb, vecB], writes=[tt])

                    def m3_n2(i):
                        rows = slice(i * 128, (i + 1) * 128)
                        xt = xr[i % 3]; x1t = x1r[i % 2]
                        P.op(dve, lambda e, xt=xt, x1t=x1t: e.tensor_tensor(x1t[:], xt[:], tt[:], ALU.add), reads=[xt, tt], writes=[x1t])
                        P.dma(sp, X1[rows, :], x1t[:], x1t, DR)
                        sb = new_stat()
                        sumsq(x1t, x1t[:], sb, 0, 1024)
                        rx = rstd_of(sb, 0, 1024)
                        P.op(dve, lambda e, x1t=x1t, rx=rx: e.scalar_tensor_tensor(h2[:], x1t[:], rx, g_pre, ALU.mult, ALU.mult),
                             reads=[x1t, sb, vecB], writes=[h2])
                        transposes(h2, lambda c: h2[:, c * 128:(c + 1) * 128], 8, pT2)
                        P.op(act, lambda e, rows=rows: e.copy(H2T[:, :, rows], pT2[:]), reads=[pT2], writes=[H2T])

                    m3_load(0); m3_load(1)
                    m3_comb(0)
                    m3_y(0)
                    if NT > 1:
                        m3_comb(1)
                    for i in range(NT):
                        if i + 2 < NT:
                            m3_load(i + 2)
                        if i + 1 < NT:
                            m3_y(i + 1)
                        P.begin_chain(); m3_n1(i); m3_n2(i); cA = P.end_chain()
                        cB = []
                        if i + 2 < NT:
                            P.begin_chain(); m3_comb(i + 2); cB = P.end_chain()
                        P.interleave(cA, cB)
                P.barrier()
                if stop_after == "M3":
                    break
                with ExitStack() as s4:
                    P.stack = s4
                    NTS = S // 512
                    yTr = [P.sbuf("yT%d" % i, [128, 32, 512], BF16) for i in range(2)]
                    NWU = 3
                    wur = [P.sbuf("wu%d" % i, [128, 8, 256], BF16) for i in range(NWU)]
                    NWD = 4
                    wdr = [P.sbuf("wd%d" % i, [128, 1024], BF16) for i in range(NWD)]
                    cpk = P.sbuf("cpk", [128, 64, 4], F32)
                    halo_t = P.sbuf("halo", [128, 2, 64, 2], F32)
                    halo = [[P.view("halo%d_%d" % (a_, j), halo_t[:, a_, j, :]) for j in range(64)] for a_ in range(2)]
                    vecC = P.sbuf("vecC", [128, 1024], F32)
                    cr = [[P.sbuf("c%d_%d" % (i, j), [128, 512], F32) for j in range(3)] for i in range(3)]
                    x1r = [P.sbuf("fx1r%d" % i, [128, DM], F32) for i in range(2)]
                    x2r = [P.sbuf("fx2r%d" % i, [128, DM], F32) for i in range(2)]
                    ftt_ = P.sbuf("ftt", [128, DM], F32)
                    B = [P.psum("B%d" % i, [128, 512], F32) for i in range(8)]
                    P.dma(sp, cpk[:], cpk_in[l], DR, cpk)
                    P.dma(sp, vecC[:], vecC_in[l], DR, vecC)
                    P.op(dve, lambda e: e.memset(halo_t[:], 0.0), writes=halo[0] + halo[1])

                    events = []
                    for w_ in range(NTS + 1):
                        dsteps = []
                        if w_ >= 1:
                            for hbk in range(2):
                                dsteps += [("D", w_ - 1, hbk, j_) for j_ in range(32)] + [("E", w_ - 1, hbk, 0)]
                        if w_ < NTS:
                            nd = 0
                            for j_ in range(32):
                                events.append(("U", w_, 0, j_))
                                tgt = ((j_ + 1) * len(dsteps)) // 32
                                while nd < tgt:
                                    events.append(dsteps[nd]); nd += 1
                        else:
                            events += dsteps
                    loads = [("u", ev[3]) for ev in events if ev[0] == "U"]
                    loads = [("u", ev[3]) if ev[0] == "U" else ("d", ev[3]) for ev in events if ev[0] in ("U", "D")]
                    lst = {"li": 0, "u": 0, "d": 0}
                    slotq = {"u": [], "d": []}
                    PF = 2

                    def ensure(upto):
                        while lst["li"] <= min(upto, len(loads) - 1):
                            kind, j_ = loads[lst["li"]]
                            if kind == "u":
                                w_ = wur[lst["u"] % NWU]; lst["u"] += 1
                                P.dma(sp, w_[:], WUb[j_], DR, w_)
                            else:
                                w_ = wdr[lst["d"] % NWD]; lst["d"] += 1
                                P.dma(sp, w_[:], WDb[j_], DR, w_)
                            slotq[kind].append(w_)
                            lst["li"] += 1
                    ci = 0
                    uu = 0
                    pendg = None
                    xi = 0

                    def emit_U(s, j):
                        nonlocal_ = None
                        tok = slice(s * 512, (s + 1) * 512)
                        yT = yTr[s % 2]
                        wu = slotq["u"].pop(0)
                        pg = B[4 + (uu_[0] % 2) * 2]; pv = B[4 + (uu_[0] % 2) * 2 + 1]
                        gc, vc, gg = cr[uu_[0] % 3]
                        for (pp, off) in ((pg, 0), (pv, 128)):
                            for c in range(8):
                                P.op(pe, lambda e, pp=pp, off=off, c=c, wu=wu, tok=tok: e.matmul(pp[:], wu[:, c, off:off + 128], H2T[:, c, tok],
                                                                                       start=(c == 0), stop=(c == 7)),
                                     reads=[wu, H2T], writes=[pp], inc=(c == 7))
                        for (pp, cc, jj) in ((pg, gc, j), (pv, vc, 32 + j)):
                            w2 = cpk[:, jj, 2:3]; cb = cpk[:, jj, 3:4]
                            hn_ = halo[(s + 1) % 2][jj]
                            P.op(act, lambda e, pp=pp, cc=cc, w2=w2, cb=cb: e.activation(cc[:], pp[:], AF.Identity, scale=w2, bias=cb),
                                 reads=[pp, cpk], writes=[cc])
                            P.op(act, lambda e, hn_=hn_, pp=pp: e.copy(hn_[:, 0:2], pp[:, 510:512]), reads=[pp], writes=[hn_])
                        for (pp, cc, jj) in ((pg, gc, j), (pv, vc, 32 + j)):
                            w0 = cpk[:, jj, 0:1]; w1 = cpk[:, jj, 1:2]
                            hl_ = halo[s % 2][jj]
                            P.op(dve, lambda e, pp=pp, cc=cc, w1=w1: e.scalar_tensor_tensor(cc[:, 1:512], pp[:, 0:511], w1, cc[:, 1:512],
                                                                                           ALU.mult, ALU.add),
                                 reads=[pp, cpk, cc], writes=[cc])
                            P.op(dve, lambda e, hl_=hl_, cc=cc, w1=w1: e.scalar_tensor_tensor(cc[:, 0:1], hl_[:, 1:2], w1, cc[:, 0:1],
                                                                                             ALU.mult, ALU.add),
                                 reads=[hl_, cpk, cc], writes=[cc])
                            P.op(dve, lambda e, pp=pp, cc=cc, w0=w0: e.scalar_tensor_tensor(cc[:, 2:512], pp[:, 0:510], w0, cc[:, 2:512],
                                                                                           ALU.mult, ALU.add),
                                 reads=[pp, cpk, cc], writes=[cc])
                            P.op(dve, lambda e, hl_=hl_, cc=cc, w0=w0: e.scalar_tensor_tensor(cc[:, 0:2], hl_[:, 0:2], w0, cc[:, 0:2],
                                                                                             ALU.mult, ALU.add),
                                 reads=[hl_, cpk, cc], writes=[cc])

                        def fin(gc=gc, gg=gg, vc=vc, j=j, yT=yT):
                            P.op(act, lambda e: e.activation(gg[:], gc[:], AF.Gelu_apprx_tanh), reads=[gc], writes=[gg])
                            P.op(pool, lambda e: e.tensor_tensor(yT[:, j, :], gg[:], vc[:], ALU.mult), reads=[gg, vc], writes=[yT])
                        if pend_[0] is not None:
                            pend_[0]()
                        pend_[0] = fin
                        uu_[0] += 1
                        if j == 31:
                            pend_[0]()
                            pend_[0] = None

                    def emit_D(s, hb, j):
                        yT = yTr[s % 2]
                        wd = slotq["d"].pop(0)
                        for tl in range(2):
                            ts = hb * 2 + tl
                            for half in range(2):
                                bb = B[tl * 2 + half]
                                P.op(pe, lambda e, bb=bb, ts=ts, half=half, wd=wd, j=j, yT=yT: e.matmul(
                                    bb[:], yT[:, j, ts * 128:(ts + 1) * 128], wd[:, half * 512:(half + 1) * 512],
                                    start=(j == 0), stop=(j == 31)),
                                    reads=[yT, wd], writes=[bb], inc=(tl == 1 and half == 1))

                    def emit_E(s, hb):
                        for tl in range(2):
                            ts = hb * 2 + tl
                            rows = slice(s * 512 + ts * 128, s * 512 + (ts + 1) * 128)
                            x1t = x1r[tl]; x2t = x2r[tl]
                            P.dma(sp, x1t[:], X1[rows, :], DR, x1t)
                            sb = new_stat()
                            b0 = B[tl * 2]; b1 = B[tl * 2 + 1]
                            P.op(act, lambda e, b0=b0, sb=sb: e.activation(junk[:, 0:512], b0[:], AF.Square, accum_out=sb[:, 8:9]),
                                 reads=[b0], writes=[sb, junk])
                            P.op(act, lambda e, b1=b1, sb=sb: e.activation(junk[:, 512:1024], b1[:], AF.Square, accum_out=sb[:, 9:10]),
                                 reads=[b1], writes=[sb, junk])
                            P.op(dve, lambda e, sb=sb: e.tensor_tensor(sb[:, 0:1], sb[:, 8:9], sb[:, 9:10], ALU.add), reads=[sb], writes=[sb])
                            rf = rstd_of(sb, 0, 1024)
                            for half, bb in ((0, b0), (1, b1)):
                                P.op(dve, lambda e, half=half, bb=bb, rf=rf: e.scalar_tensor_tensor(
                                    ftt_[:, half * 512:(half + 1) * 512], bb[:], rf, vecC[:, half * 512:(half + 1) * 512], ALU.mult, ALU.mult),
                                    reads=[bb, sb, vecC], writes=[ftt_])
                            P.op(dve, lambda e, x1t=x1t, x2t=x2t: e.tensor_tensor(x2t[:], x1t[:], ftt_[:], ALU.add), reads=[x1t, ftt_], writes=[x2t])
                            P.dma(sp, Xdst[rows, :], x2t[:], x2t, DR)

                    uu_ = [0]
                    pend_ = [None]
                    for ev in events:
                        if ev[0] == "U":
                            ensure(ci + PF); ci += 1
                            emit_U(ev[1], ev[3])
                        elif ev[0] == "D":
                            ensure(ci + PF); ci += 1
                            emit_D(ev[1], ev[2], ev[3])
                        else:
                            emit_E(ev[1], ev[2])
                P.barrier()
            P.stack = st
        P.stack = st
        P.barrier()
        block = st.enter_context(nc.Block())
        P.emit(block)
    return nc


def _prep_inputs(inp):
    f = np.float32
    L = 2
    g = {k: np.asarray(v, dtype=f) for k, v in inp.items()}
    rep = lambda v: np.ascontiguousarray(np.broadcast_to(v[:, None, :], (L, 128, v.shape[-1])))
    shared = {}
    shared["w_in"] = np.ascontiguousarray(g["w_in"].reshape(L, 8, 128, 2560).transpose(0, 2, 1, 3))
    shared["w_out"] = np.ascontiguousarray(g["w_out"].reshape(L, 8, 128, 1024).transpose(0, 2, 1, 3))
    wu = g["w_up"].reshape(L, 8, 128, 2, 32, 128)
    shared["w_up"] = np.ascontiguousarray(wu.transpose(0, 4, 2, 1, 3, 5).reshape(L, 32, 128, 8, 256))
    shared["w_down"] = np.ascontiguousarray(g["w_down"].reshape(L, 32, 128, 1024))
    shared["wsT"] = np.ascontiguousarray(g["w_spatial"].transpose(0, 3, 1, 2))
    shared["bsp"] = np.ascontiguousarray(g["b_spatial"].transpose(0, 2, 1))
    shared["vecA"] = rep(np.concatenate([g["pre_mix_norm"], g["v_norm_g"], g["v_norm_b"], g["out_norm_a"]], -1))
    shared["vecB"] = rep(np.concatenate([g["out_norm_b"], g["post_mix_norm"], g["pre_ffn_norm"]], -1))
    shared["vecC"] = rep(g["post_ffn_norm"])
    cp = np.concatenate([g["conv_w"], g["conv_b"][:, None, :]], 1)
    shared["cpk"] = np.ascontiguousarray(cp.reshape(L, 4, 64, 128).transpose(0, 3, 2, 1))
    t = np.arange(S, dtype=f)
    inv = (f(500000.0) ** (-np.arange(0, 16, 2, dtype=f) / f(16))).astype(f)
    ang = (t[:, None] * inv[None, :]).astype(f)
    cs = np.stack([np.cos(ang), np.sin(ang)], 0).astype(f)
    shared["rope"] = np.ascontiguousarray(cs.reshape(2, NT, 128, 8).transpose(2, 0, 1, 3))
    k = np.arange(128)[:, None]; q = np.arange(128)[None, :]
    cur = (k <= q).astype(f); prev = (k >= q).astype(f)
    shared["cst"] = np.ascontiguousarray(np.concatenate([np.eye(128, dtype=f), cur, cur, prev, cur, prev], 1))
    return g["x"], shared


_NC_CACHE = {}


def kernel(**inputs):
    x, shared = _prep_inputs(inputs)
    if "nc" not in _NC_CACHE:
        _NC_CACHE["nc"] = build_program()
    nc = _NC_CACHE["nc"]
    in_maps = []
    for c in range(NCORES):
        m = dict(shared)
        m["x"] = np.ascontiguousarray(x[c])
        in_maps.append(m)
    res = run_bass_kernel_spmd(nc, in_maps, core_ids=list(range(NCORES)))
    return np.stack([np.asarray(r["out"], dtype=np.float32) for r in res.results], 0)
```

```python
import numpy as np
import concourse.bass as bass
import concourse.mybir as mybir
from contextlib import ExitStack

F32 = mybir.dt.float32
BF16 = mybir.dt.bfloat16
AF = mybir.ActivationFunctionType
ALU = mybir.AluOpType
AX = mybir.AxisListType


class Eng:
    def __init__(self, fw, name, e, sem):
        self.fw = fw
        self.name = name
        self.e = e
        self.sem = sem
        self.count = 0
        self.waited = {}
        self.pending = False
        self.prog = []
        self.cur_waits = []

    def wait(self, sem, val):
        if sem is self.sem and self.name == "pe":
            return
        key = id(sem)
        if self.waited.get(key, 0) >= val:
            return
        self.waited[key] = val
        self.cur_waits.append((sem, val))

    def push(self, fn, incs):
        self.prog.append((self.cur_waits, fn, incs))
        self.cur_waits = []

    def replay(self, e):
        for waits, fn, incs in self.prog:
            for sem, val in waits:
                e.wait_ge(sem, val)
            if fn is None:
                continue
            ins = fn(e)
            for sem, n in incs:
                ins.then_inc(sem, n)


class Buf:
    def __init__(self, fw, name, ap=None, is_dram=False):
        self.fw = fw
        self.name = name
        self.ap = ap
        self.is_dram = is_dram
        self.writers = []
        self.readers = []
        self.dsem = None
        self.dcount = 0
        fw.all_bufs.append(self)

    def __getitem__(self, idx):
        return self.ap[idx]

    def dma_sem(self, kind="hw"):
        if self.dsem is None:
            self.dkind = kind
            self.dsem, self.dcount = self.fw.take_dsem(self.name, kind)
            self.fw.dbufs.append(self)
        assert self.dkind == kind, ("mixed SW/HW DGE on one semaphore", self.name)
        return self.dsem


class FW:
    def __init__(self, nc, stack):
        self.nc = nc
        self.stack = stack
        self.sem_stack = stack
        self.uid = 0
        self.nsem = 0
        self.engs = {}
        self.all_bufs = []
        for name, e in (("pe", nc.tensor), ("act", nc.scalar), ("dve", nc.vector),
                        ("pool", nc.gpsimd), ("sp", nc.sync)):
            self.engs[name] = Eng(self, name, e, self.new_sem("e_" + name))
        self.pe = self.engs["pe"]; self.act = self.engs["act"]; self.dve = self.engs["dve"]
        self.pool = self.engs["pool"]; self.sp = self.engs["sp"]
        self.capture = None
        self.arena = None
        self.cur_atom = []
        self.sem_pool = {}
        self.dbufs = []

    def take_dsem(self, name, kind):
        pool = self.sem_pool.setdefault(kind, [])
        if pool:
            return pool.pop()
        return self.new_sem("d_%s_%s" % (kind, name)), 0

    def barrier(self):
        engs = list(self.engs.values())
        for e in engs:
            for f in engs:
                if f is not e and f.count > 0:
                    e.wait(f.sem, f.count)
            for b in self.dbufs:
                if b.dcount > 0:
                    e.wait(b.dsem, b.dcount)
            e.push(None, [])
        for b in self.dbufs:
            self.sem_pool.setdefault(b.dkind, []).append((b.dsem, b.dcount))
            b.dsem = None
            b.dcount = 0
        self.dbufs = []
        for b in self.all_bufs:
            b.writers = []
            b.readers = []

    def new_sem(self, name):
        self.nsem += 1
        name = "%s_s%d" % (name, self.nsem)
        return self.sem_stack.enter_context(self.nc.semaphore(name))

    def init_arena(self):
        nbytes = (self.nc.sbuf_bytes_remaining // 64) * 64 - 128
        self.arena = self.sem_stack.enter_context(self.nc.sbuf_tensor("arena", [128, nbytes], mybir.dt.uint8))
        self.sb_off = 0
        self.sb_cap = nbytes
        self.parena = self.sem_stack.enter_context(self.nc.psum_tensor("parena", [128, 8, 512], F32))
        self.ps_off = 0

    @staticmethod
    def _reshape(ap, shape):
        if len(shape) == 2:
            return ap
        if len(shape) == 3:
            return ap.rearrange("p (a b) -> p a b", a=shape[1])
        if len(shape) == 4:
            return ap.rearrange("p (a b c) -> p a b c", a=shape[1], b=shape[2])
        raise ValueError(shape)

    def _sb_release(self, off):
        self.sb_off = off

    def _ps_release(self, off):
        self.ps_off = off

    def sbuf(self, name, shape, dtype):
        if self.arena is None:
            self.init_arena()
        assert shape[0] == 128
        esz = 4 if dtype == F32 else 2
        n = 1
        for d in shape[1:]:
            n *= d
        nb = n * esz
        nb_al = (nb + 63) // 64 * 64
        off = self.sb_off
        assert off + nb_al <= self.sb_cap, ("SBUF arena overflow", name, off, nb_al, self.sb_cap)
        self.sb_off = off + nb_al
        self.stack.callback(self._sb_release, off)
        ap = self.arena[:, off:off + nb].bitcast(dtype)
        return Buf(self, name, self._reshape(ap, shape))

    def psum(self, name, shape, dtype=F32):
        if self.arena is None:
            self.init_arena()
        assert shape[0] == 128
        esz = 4 if dtype == F32 else 2
        n = 1
        for d in shape[1:]:
            n *= d
        nb = n * esz
        banks = (nb + 2047) // 2048
        b0 = self.ps_off
        assert b0 + banks <= 8, ("PSUM arena overflow", name)
        self.ps_off = b0 + banks
        self.stack.callback(self._ps_release, b0)
        if banks == 1:
            ap = self.parena[:, b0, :]
        else:
            ap = self.parena[:, b0:b0 + banks, :].rearrange("p b c -> p (b c)")
        if dtype != F32:
            ap = ap.bitcast(dtype)
        ap = ap[:, 0:n]
        return Buf(self, name, self._reshape(ap, shape))

    def dram(self, name, ap=None):
        return Buf(self, name, ap, is_dram=True)

    def view(self, name, ap):
        return Buf(self, name, ap)

    def _wait_set(self, eng, tickets):
        best = {}
        for sem, val in tickets:
            k = id(sem)
            if k not in best or best[k][1] < val:
                best[k] = (sem, val)
        for sem, val in best.values():
            eng.wait(sem, val)

    def _deps(self, eng, reads, writes):
        t = []
        for b in reads:
            t += b.writers
        for b in writes:
            t += b.writers
            t += b.readers
        self._wait_set(eng, t)

    def _record(self, ticket, reads, writes, append_write=False):
        for b in reads:
            b.readers.append(ticket)
            if len(b.readers) > 12:
                b.readers = _compact(b.readers)
        for b in writes:
            if append_write or b.is_dram:
                b.writers.append(ticket)
                if len(b.writers) > 12:
                    b.writers = _compact(b.writers)
            else:
                b.writers = [ticket]
                b.readers = []

    def op(self, eng, fn, reads=(), writes=(), inc=True):
        if self.capture is not None:
            self.cur_atom.append(("op", (eng, fn, reads, writes, inc)))
            if inc:
                self.capture.append(self.cur_atom)
                self.cur_atom = []
            return
        self._deps(eng, reads, writes)
        ticket = (eng.sem, eng.count + 1)
        if inc:
            eng.push(fn, [(eng.sem, 1)])
            eng.count += 1
            eng.pending = False
        else:
            eng.push(fn, [])
            eng.pending = True
        self._record(ticket, reads, writes)

    def dma(self, eng, out, in_, src, dst, owner=None, same_fill=False, **kw):
        if owner is None:
            owner = dst if not dst.is_dram else src
        if self.capture is not None:
            self.cur_atom.append(("dma", (eng, out, in_, src, dst, owner, same_fill, kw)))
            self.capture.append(self.cur_atom)
            self.cur_atom = []
            return
        t = []
        if not src.is_dram:
            t += src.writers
        if not dst.is_dram:
            if not same_fill:
                t += dst.writers
            t += dst.readers
        self._wait_set(eng, t)
        sem = owner.dma_sem("sw" if eng.name == "pool" else "hw")
        eng.push(lambda e: e.dma_start(out=out, in_=in_, **kw), [(sem, 16)])
        owner.dcount += 16
        ticket = (sem, owner.dcount)
        self._record(ticket, [] if src.is_dram else [src], [] if dst.is_dram else [dst], append_write=same_fill)

    def begin_chain(self):
        assert self.capture is None
        self.capture = []
        self.cur_atom = []

    def end_chain(self):
        assert not self.cur_atom
        c = self.capture
        self.capture = None
        return c

    def interleave(self, *chains):
        chains = [list(c) for c in chains]
        while any(chains):
            for c in chains:
                if c:
                    for kind, a in c.pop(0):
                        if kind == "op":
                            self.op(a[0], a[1], reads=a[2], writes=a[3], inc=a[4])
                        else:
                            self.dma(a[0], a[1], a[2], a[3], a[4], owner=a[5], same_fill=a[6], **a[7])

    def wait_all(self, eng, bufs):
        for b in bufs:
            for w in b.writers:
                eng.wait(*w)
            for r in b.readers:
                eng.wait(*r)
        eng.push(None, [])

    def emit(self, block):
        block.gpsimd(lambda e: self.pool.replay(e))
        block.sync(lambda e: self.sp.replay(e))
        block.tensor(lambda e: self.pe.replay(e))
        block.vector(lambda e: self.dve.replay(e))
        block.scalar(lambda e: self.act.replay(e))

    def dram_reset(self, b):
        b.writers = []
        b.readers = []


def _compact(readers):
    best = {}
    for sem, val in readers:
        k = id(sem)
        if k not in best or best[k][1] < val:
            best[k] = (sem, val)
    return list(best.values())


from concourse.bass_utils import run_bass_kernel_spmd

S = 4096
DM = 1024
NT = S // 128
EPS = 1e-6
PATTERNS = (1, 4, 16)
NCORES = 8
CUT = 99
FSUB = 9
M2S = 9
M2N = 10**9
SUB = 2
NTL = NT


class _FullAP:
    def __init__(self, ap, full):
        self._ap = ap
        self.tensor_full = full

    def __getitem__(self, idx):
        return self._ap[idx]


def build_program(n_layers=2, stop_after=None, debug=False):
    nc = bass.Bass("TRN2", target_bir_lowering=False)
    IN = lambda name, shape, dt=F32: nc.dram_tensor(name, list(shape), dt, kind="ExternalInput").ap()
    dbgkind = "ExternalOutput" if debug else "Internal"
    SCR = lambda name, shape, dt: nc.dram_tensor(name, list(shape), dt, kind=dbgkind).ap()

    x_in = IN("x", [S, DM])
    w_in = IN("w_in", [2, 128, 8, 2560])
    w_out = IN("w_out", [2, 128, 8, 1024])
    w_up = IN("w_up", [2, 32, 128, 8, 256])
    w_down = IN("w_down", [2, 32, 128, 1024])
    wsT_in = IN("wsT", [2, 128, 4, 128])
    bsp_in = IN("bsp", [2, 128, 4])
    vecA_in = IN("vecA", [2, 128, 2560])
    vecB_in = IN("vecB", [2, 128, 2560])
    vecC_in = IN("vecC", [2, 128, 1024])
    cpk_in = IN("cpk", [2, 128, 64, 4])
    rope_in = IN("rope", [128, 2, NT, 8])
    cst_in = IN("cst", [128, 768])
    out = nc.dram_tensor("out", [S, DM], F32, kind="ExternalOutput").ap()

    Vs = SCR("Vs", [S, 520], BF16)
    MA = SCR("MA", [S, 512], BF16)
    Od = [SCR("O%d" % d, [S, 520], F32) for d in PATTERNS]
    X1 = SCR("X1", [S, DM], F32)
    XA = SCR("XA", [S, DM], F32)
    WUb = nc.dram_tensor("WUb", [32, 128, 8, 256], BF16).ap()
    WDb = nc.dram_tensor("WDb", [32, 128, 1024], BF16).ap()

    with ExitStack() as st:
        P = FW(nc, st)
        pe, act, dve, pool, sp = P.pe, P.act, P.dve, P.pool, P.sp
        DR = P.dram("dram")
        cvt = [Buf(P, "cvt%d" % i) for i in range(4)]

        ident = P.sbuf("ident", [128, 128], BF16)
        mask4 = P.sbuf("mask4", [128, 512], BF16)
        triu = P.sbuf("triu", [128, 128], F32)
        rope = P.sbuf("rope", [128, 2, NT, 8], F32)
        mhalf = P.sbuf("mhalf", [128, 1], F32)
        junk = P.sbuf("junk", [128, 1024], BF16)
        NSTAT = 16
        stats = [P.sbuf("stat%d" % i, [128, 16], F32) for i in range(NSTAT)]
        stat_i = [0]

        def new_stat():
            b = stats[stat_i[0] % NSTAT]
            stat_i[0] += 1
            return b

        P.dma(pool, ident[:], cst_in[:, 0:128], DR, ident)
        P.dma(pool, mask4[:], cst_in[:, 256:768], DR, mask4)
        P.dma(sp, triu[:], cst_in[:, 128:256], DR, triu)
        P.dma(sp, rope[:], rope_in, DR, rope)
        P.op(dve, lambda e: e.memset(mhalf[:], -0.5), writes=[mhalf])

        def rstd_of(sb, col, n):
            P.op(dve, lambda e: e.tensor_scalar(sb[:, col + 1:col + 2], sb[:, col:col + 1], 1.0 / n, EPS, ALU.mult, ALU.add),
                 reads=[sb], writes=[sb])
            P.op(pool, lambda e: e.tensor_tensor(sb[:, col + 2:col + 3], sb[:, col + 1:col + 2], mhalf[:], ALU.pow),
                 reads=[sb, mhalf], writes=[sb])
            return sb[:, col + 2:col + 3]

        def sumsq(src_buf, src_ap, sb, col, width):
            P.op(act, lambda e: e.activation(junk[:, 0:width], src_ap, AF.Square, accum_out=sb[:, col:col + 1]),
                 reads=[src_buf], writes=[sb, junk])

        def transposes(src_buf, src_ap_fn, n, pt_buf):
            for c in range(n):
                P.op(pe, lambda e, c=c: e.transpose(pt_buf[:, c, :], src_ap_fn(c), ident[:]),
                     reads=[src_buf, ident], writes=[pt_buf], inc=(c == n - 1))

        for l in range(n_layers):
            Xsrc = x_in if l == 0 else XA
            Xdst = out if l == n_layers - 1 else XA
            with ExitStack() as sA:
                P.stack = sA
                QT = P.sbuf("QT", [128, 4, S], BF16)
                KT = P.sbuf("KT", [128, 4, S], BF16)
                with ExitStack() as s1:
                    P.stack = s1
                    win = P.sbuf("win", [128, 8, 2560], BF16)
                    vecA = P.sbuf("vecA", [128, 2560], F32)
                    wsT = P.sbuf("wsT", [128, 4, 128], BF16)
                    wsf = P.sbuf("wsf", [128, 4, 128], F32)
                    bsp = P.sbuf("bsp", [128, 4], F32)
                    xr = [P.sbuf("xr%d" % i, [128, DM], F32) for i in range(3)]
                    hb = P.sbuf("hb", [128, DM], BF16)
                    hT = [P.sbuf("hT%d" % i, [128, 8, 128], BF16) for i in range(2)]
                    gu = P.sbuf("gu", [128, 512], F32)
                    gv = P.sbuf("gv", [128, 512], F32)
                    vn = P.sbuf("vn", [128, 512], F32)
                    vn3 = P.sbuf("vn3", [128, 512], BF16)
                    oa = P.sbuf("oa", [128, 512], F32)
                    mar = [P.sbuf("mar%d" % i, [128, 512], BF16) for i in range(2)]
                    qk = P.sbuf("qk", [128, 16, 64], BF16)
                    tA = P.sbuf("tA", [128, 16, 8], F32)
                    tB = P.sbuf("tB", [128, 16, 8], F32)
                    vtr = [P.sbuf("vtr%d" % i, [128, 8, 65], BF16) for i in range(2)]
                    bnst = P.sbuf("bnst", [128, 8], F32)
                    pT = P.psum("pT", [128, 8, 128], BF16)
                    pA = P.psum("pA", [128, 1024], F32)
                    pB = P.psum("pB", [128, 1024], F32)
                    pC = P.psum("pC", [128, 512], F32)
                    psg = P.psum("psg", [128, 512], F32)
                    pqt = P.psum("pqt", [128, 8, 128], BF16)
                    pAu = P.view("pAu", pA[:, 0:512]); pAv = P.view("pAv", pA[:, 512:1024])

                    for c4 in range(4):
                        P.dma(pool, win[:, 2 * c4:2 * c4 + 2, :], w_in[l, :, 2 * c4:2 * c4 + 2, :], DR, win, same_fill=True)
                    P.dma(sp, vecA[:], vecA_in[l], DR, vecA)
                    P.dma(sp, wsf[:], wsT_in[l], DR, wsf)
                    P.dma(sp, bsp[:], bsp_in[l], DR, bsp)
                    for g in range(4):
                        P.op(dve, lambda e, g=g: e.tensor_tensor(wsT[:, g, :], wsf[:, g, :], triu[:], ALU.mult),
                             reads=[wsf, triu], writes=[wsT])
                    for v in vtr:
                        P.op(dve, lambda e, v=v: e.memset(v[:, :, 64:65], 1.0), writes=[v])
                    g_pm = vecA[:, 0:1024]; g_vg = vecA[:, 1024:1536]; g_vb = vecA[:, 1536:2048]; g_oa = vecA[:, 2048:2560]

                    def m1_load(i):
                        xt = xr[i % 3]
                        rows = slice(i * 128, (i + 1) * 128)
                        P.dma(sp, xt[:], Xsrc[rows, :], DR, xt)
                        P.dma(pool, WUb[i], w_up[l, i], DR, DR, owner=cvt[i % 4])

                    def m1_head(i):
                        xt = xr[i % 3]; hTi = hT[i % 2]
                        sb = new_stat()
                        sumsq(xt, xt[:], sb, 0, 1024)
                        r0 = rstd_of(sb, 0, 1024)
                        P.op(dve, lambda e, xt=xt, r0=r0: e.scalar_tensor_tensor(hb[:], xt[:], r0, g_pm, ALU.mult, ALU.mult),
                             reads=[xt, sb, vecA], writes=[hb])
                        transposes(hb, lambda c: hb[:, c * 128:(c + 1) * 128], 8, pT)
                        P.op(act, lambda e, hTi=hTi: e.copy(hTi[:], pT[:]), reads=[pT], writes=[hTi])

                    def m1_proj(i):
                        hTi = hT[i % 2]
                        for ct, (pbuf, pap) in enumerate(((pAu, pA[:, 0:512]), (pAv, pA[:, 512:1024]), (pB, pB[:, 0:512]),
                                                          (pB, pB[:, 512:1024]), (pC, pC[:]))):
                            for c in range(8):
                                P.op(pe, lambda e, c=c, ct=ct, pap=pap, hTi=hTi: e.matmul(
                                    pap, hTi[:, c, :], win[:, c, ct * 512:(ct + 1) * 512], start=(c == 0), stop=(c == 7)),
                                    reads=[hTi, win], writes=[pbuf], inc=(c == 7))

                    def m1_gelu(i):
                        P.op(act, lambda e: e.activation(gu[:], pA[:, 0:512], AF.Gelu), reads=[pAu], writes=[gu])
                        P.op(act, lambda e: e.activation(gv[:], pA[:, 512:1024], AF.Gelu), reads=[pAv], writes=[gv])

                    def m1_rope(i):
                        vt = vtr[i % 2]
                        rows = slice(i * 128, (i + 1) * 128)
                        pBv = pB[:].rearrange("p (h d) -> p h d", d=64)
                        cosb = rope[:, 0, i, :].unsqueeze(1).to_broadcast([128, 16, 8])
                        sinb = rope[:, 1, i, :].unsqueeze(1).to_broadcast([128, 16, 8])
                        x1 = pBv[:, :, 0:8]; x2 = pBv[:, :, 8:16]
                        P.op(dve, lambda e, x1=x1, cosb=cosb: e.tensor_tensor(tA[:], x1, cosb, ALU.mult), reads=[pB, rope], writes=[tA])
                        P.op(dve, lambda e, x2=x2, sinb=sinb: e.tensor_tensor(tB[:], x2, sinb, ALU.mult), reads=[pB, rope], writes=[tB])
                        P.op(dve, lambda e: e.tensor_tensor(qk[:, :, 0:8], tA[:], tB[:], ALU.subtract), reads=[tA, tB], writes=[qk])
                        P.op(dve, lambda e, x2=x2, cosb=cosb: e.tensor_tensor(tA[:], x2, cosb, ALU.mult), reads=[pB, rope], writes=[tA])
                        P.op(dve, lambda e, x1=x1, sinb=sinb: e.tensor_tensor(tB[:], x1, sinb, ALU.mult), reads=[pB, rope], writes=[tB])
                        P.op(dve, lambda e: e.tensor_tensor(qk[:, :, 8:16], tA[:], tB[:], ALU.add), reads=[tA, tB], writes=[qk])
                        P.op(act, lambda e, pBv=pBv: e.copy(qk[:, :, 16:64], pBv[:, :, 16:64]), reads=[pB], writes=[qk])
                        P.op(act, lambda e, vt=vt: e.copy(vt[:, :, 0:64], pC[:].rearrange("p (h d) -> p h d", d=64)),
                             reads=[pC], writes=[vt])
                        P.dma(sp, Vs[rows, :], vt[:].rearrange("p h d -> p (h d)"), vt, DR)

                    def m1_ln(i):
                        sb = new_stat()
                        P.op(dve, lambda e: e.bn_stats(bnst[:, 0:6], gv[:]), reads=[gv], writes=[bnst])
                        P.op(dve, lambda e, sb=sb: e.bn_aggr(sb[:, 4:6], bnst[:, 0:6]), reads=[bnst], writes=[sb])
                        rv = rstd_of(sb, 5, 1.0)
                        P.op(dve, lambda e, sb=sb, rv=rv: e.tensor_scalar(vn[:], gv[:], sb[:, 4:5], rv, ALU.subtract, ALU.mult),
                             reads=[gv, sb], writes=[vn])
                        P.op(dve, lambda e: e.tensor_tensor(vn[:], vn[:], g_vg, ALU.mult), reads=[vn, vecA], writes=[vn])
                        P.op(dve, lambda e: e.tensor_tensor(vn3[:], vn[:], g_vb, ALU.add), reads=[vn, vecA], writes=[vn3])
                        for g in range(4):
                            P.op(pe, lambda e, g=g: e.matmul(psg[:, g * 128:(g + 1) * 128], wsT[:, g, :], vn3[:, g * 128:(g + 1) * 128],
                                                             start=True, stop=True),
                                 reads=[wsT, vn3], writes=[psg], inc=(g == 3))

                    def m1_oa(i):
                        ma = mar[i % 2]
                        rows = slice(i * 128, (i + 1) * 128)
                        for g in range(4):
                            P.op(dve, lambda e, g=g: e.scalar_tensor_tensor(oa[:, g * 128:(g + 1) * 128], psg[:, g * 128:(g + 1) * 128],
                                                                             bsp[:, g:g + 1], gu[:, g * 128:(g + 1) * 128], ALU.add, ALU.mult),
                                 reads=[psg, bsp, gu], writes=[oa])
                        sb = new_stat()
                        sumsq(oa, oa[:], sb, 0, 512)
                        ra = rstd_of(sb, 0, 512)
                        P.op(dve, lambda e, ma=ma, ra=ra: e.scalar_tensor_tensor(ma[:], oa[:], ra, g_oa, ALU.mult, ALU.mult),
                             reads=[oa, sb, vecA], writes=[ma])
                        P.dma(sp, MA[rows, :], ma[:], ma, DR)

                    def m1_qkT(i):
                        rows = slice(i * 128, (i + 1) * 128)
                        qkf = qk[:].rearrange("p h d -> p (h d)")
                        transposes(qk, lambda c, qkf=qkf: qkf[:, c * 128:(c + 1) * 128], 8, pqt)
                        P.op(act, lambda e, rows=rows: e.copy(QT[:, :, rows], pqt[:, 0:4, :]), reads=[pqt], writes=[QT])
                        P.op(act, lambda e, rows=rows: e.copy(KT[:, :, rows], pqt[:, 4:8, :]), reads=[pqt], writes=[KT])

                    m1_load(0); m1_load(1)
                    m1_head(0)
                    for i in range(NT):
                        if i + 2 < NT:
                            m1_load(i + 2)
                        m1_proj(i)
                        P.begin_chain(); m1_gelu(i); m1_ln(i); m1_oa(i); cA = P.end_chain()
                        P.begin_chain(); m1_rope(i); m1_qkT(i); cC = P.end_chain()
                        cB = []
                        if i + 1 < NT:
                            P.begin_chain(); m1_head(i + 1); cB = P.end_chain()
                        P.interleave(cA, cC, cB)
                P.barrier()
                if stop_after == "M1":
                    break
                with ExitStack() as s2:
                    P.stack = s2
                    NVA = 3
                    var_ = [P.sbuf("va%d" % i, [128, 8, 65], BF16) for i in range(NVA)]
                    NPT = 3
                    praw = [P.sbuf("praw%d" % i, [128, 1024], BF16) for i in range(NPT)]
                    ptb = [P.sbuf("pt%d" % i, [128, 1024], BF16) for i in range(NPT)]
                    obr = [P.sbuf("ob%d" % i, [128, 2, 260], F32) for i in range(2)]
                    Sb = [P.psum("S%d" % i, [128, 2, 512], F32) for i in range(2)]
                    pO = [P.psum("pO%d" % i, [128, 2, 512], F32) for i in range(2)]
                    units = []
                    vi = 0
                    bi = 0
                    for pi, d in enumerate(PATTERNS):
                        for r in range(d):
                            prev_slot = None
                            for n in range(S // (128 * d)):
                                start = r + d * 128 * n
                                sl = slice(start, start + d * 127 + 1, d)
                                psl = slice(start - d * 128, start - d + 1, d)
                                va = var_[vi % NVA]; vi += 1
                                units.append(("load", va, sl))
                                for hpp in range(2):
                                    units.append(("unit", pi, sl, psl, n, hpp, va, prev_slot, bi))
                                prev_slot = va
                                bi += 1
                    v3d = lambda t: t[:].rearrange("p (a c) -> p a c", a=2)
                    v4c = lambda t3: t3.rearrange("p a (q w c) -> p a q w c", q=2, w=2)[:, :, :, 0, :]
                    mask3 = mask4[:].unsqueeze(1).to_broadcast([128, 2, 512])
                    mask4c = mask4[:].rearrange("p (q w c) -> p q w c", q=2, w=2)[:, :, 0, :].unsqueeze(1).to_broadcast([128, 2, 2, 128])

                    def emit_qk(u, k):
                        _, pi, sl, psl, n, hpp, va, vprev, b = u
                        Sk = Sb[k % 2]
                        nw = 2 if n > 0 else 1
                        nmm = 4 * nw
                        j = 0
                        for q in range(2):
                            hp = 2 * hpp + q
                            for hl in range(2):
                                for w in range(nw):
                                    ksl = sl if w == 0 else psl
                                    c0 = (q * 2 + w) * 128
                                    j += 1
                                    P.op(pe, lambda e, Sk=Sk, c0=c0, hl=hl, ksl=ksl, hp=hp: e.matmul(
                                        Sk[:, hl, c0:c0 + 128], KT[hl * 64:(hl + 1) * 64, hp, ksl], QT[hl * 64:(hl + 1) * 64, hp, sl],
                                        start=True, stop=True), reads=[KT, QT], writes=[Sk], inc=(j == nmm))

                    def emit_exp(u, k):
                        _, pi, sl, psl, n, hpp, va, vprev, b = u
                        Sk = Sb[k % 2]; pr = praw[k % NPT]; pt = ptb[k % NPT]
                        if n > 0:
                            P.op(act, lambda e: e.activation(v3d(pr), Sk[:], AF.Exp, scale=0.125), reads=[Sk], writes=[pr])
                            P.op(dve, lambda e: e.tensor_tensor(v3d(pt), v3d(pr), mask3, ALU.mult), reads=[pr, mask4], writes=[pt])
                        else:
                            P.op(act, lambda e: e.activation(v4c(v3d(pr)), v4c(Sk[:]), AF.Exp, scale=0.125), reads=[Sk], writes=[pr])
                            P.op(dve, lambda e: e.tensor_tensor(v4c(v3d(pt)), v4c(v3d(pr)), mask4c, ALU.mult), reads=[pr, mask4], writes=[pt])

                    def emit_pv(u, k):
                        _, pi, sl, psl, n, hpp, va, vprev, b = u
                        pt = ptb[k % NPT]; po = pO[b % 2]
                        pt3 = v3d(pt)
                        for q in range(2):
                            for hl in range(2):
                                h = (2 * hpp + q) * 2 + hl
                                oap = po[:, h // 4, (h % 4) * 65:(h % 4) * 65 + 65]
                                last = (q == 1 and hl == 1)
                                c0 = (q * 2) * 128
                                P.op(pe, lambda e, oap=oap, hl=hl, h=h, c0=c0: e.matmul(oap, pt3[:, hl, c0:c0 + 128], va[:, h, :],
                                                                                       start=True, stop=(n == 0)),
                                     reads=[pt, va], writes=[po], inc=(last and n == 0))
                                if n > 0:
                                    P.op(pe, lambda e, oap=oap, hl=hl, h=h, c0=c0: e.matmul(oap, pt3[:, hl, c0 + 128:c0 + 256],
                                                                                           vprev[:, h, :], start=False, stop=True),
                                         reads=[pt, vprev], writes=[po], inc=last)
                        if hpp == 1:
                            ob = obr[b % 2]
                            P.op(dve, lambda e: e.tensor_copy(ob[:], po[:, :, 0:260]), reads=[po], writes=[ob])
                            P.dma(sp, Od[pi][sl, :], ob[:].rearrange("p a c -> p (a c)"), ob, DR)

                    pend = None
                    k = 0
                    for u in units:
                        if k >= M2N:
                            break
                        if u[0] == "load":
                            P.dma(sp, u[1][:].rearrange("p h d -> p (h d)"), Vs[u[2], :], DR, u[1])
                            continue
                        if M2S >= 1:
                            emit_qk(u, k)
                        if M2S >= 2:
                            emit_exp(u, k)
                        if pend is not None and M2S >= 4:
                            emit_pv(*pend)
                        pend = (u, k)
                        k += 1
                    if M2S >= 4:
                        emit_pv(*pend)
                P.barrier()
            P.stack = st
            if stop_after in ("M1", "M2"):
                break
            with ExitStack() as sB:
                P.stack = sB
                H2T = P.sbuf("H2T", [128, 8, S], BF16)
                with ExitStack() as s3:
                    P.stack = s3
                    wout = P.sbuf("wout", [128, 8, 1024], BF16)
                    vecB = P.sbuf("vecB", [128, 2560], F32)
                    g_ob = vecB[:, 0:512]; g_post = vecB[:, 512:1536]; g_pre = vecB[:, 1536:2560]
                    otr = [[P.sbuf("ot%d_%d" % (i, j), [128, 8, 65], F32) for j in range(3)] for i in range(3)]
                    mx = [P.sbuf("mx%d" % i, [128, 1024], BF16) for i in range(3)]
                    xr = [P.sbuf("xr%d" % i, [128, DM], F32) for i in range(3)]
                    osum = P.sbuf("osum", [128, 8, 65], F32)
                    rl = P.sbuf("rl", [128, 8], F32)
                    obf = P.sbuf("obf", [128, 512], F32)
                    mTr = [P.sbuf("mT%d" % i, [128, 8, 128], BF16) for i in range(2)]
                    tt = P.sbuf("tt", [128, DM], F32)
                    x1r = [P.sbuf("x1r%d" % i, [128, DM], F32) for i in range(2)]
                    h2 = P.sbuf("h2", [128, DM], BF16)
                    pT = P.psum("pT3", [128, 8, 128], BF16)
                    pyr = [P.psum("py%d" % i, [128, 1024], F32) for i in range(2)]
                    pT2 = P.psum("pT4", [128, 8, 128], BF16)
                    P.dma(pool, wout[:], w_out[l], DR, wout)
                    P.dma(sp, vecB[:], vecB_in[l], DR, vecB)
                    def m3_load(i):
                        rows = slice(i * 128, (i + 1) * 128)
                        ot = otr[i % 3]; m = mx[i % 3]; xt = xr[i % 3]
                        for j in range(3):
                            P.dma(sp, ot[j][:].rearrange("p h d -> p (h d)"), Od[j][rows, :], DR, ot[j])
                        P.dma(sp, m[:, 0:512], MA[rows, :], DR, m)
                        P.dma(sp, xt[:], Xsrc[rows, :], DR, xt)
                        P.dma(pool, WDb[i], w_down[l, i], DR, DR, owner=cvt[i % 4])

                    def m3_comb(i):
                        ot = otr[i % 3]; m = mx[i % 3]; xt = xr[i % 3]
                        P.op(dve, lambda e, ot=ot: e.tensor_tensor(osum[:], ot[0][:], ot[1][:], ALU.add), reads=[ot[0], ot[1]], writes=[osum])
                        P.op(dve, lambda e, ot=ot: e.tensor_tensor(osum[:], osum[:], ot[2][:], ALU.add), reads=[osum, ot[2]], writes=[osum])
                        P.op(dve, lambda e: e.reciprocal(rl[:], osum[:, :, 64]), reads=[osum], writes=[rl])
                        P.op(dve, lambda e: e.tensor_tensor(obf[:].rearrange("p (h d) -> p h d", d=64), osum[:, :, 0:64],
                                                            rl[:].unsqueeze(2).to_broadcast([128, 8, 64]), ALU.mult),
                             reads=[osum, rl], writes=[obf])
                        sb = new_stat()
                        sumsq(obf, obf[:], sb, 0, 512)
                        rb = rstd_of(sb, 0, 512)
                        P.op(dve, lambda e, m=m, rb=rb: e.scalar_tensor_tensor(m[:, 512:1024], obf[:], rb, g_ob, ALU.mult, ALU.mult),
                             reads=[obf, sb, vecB], writes=[m])
                        transposes(m, lambda c, m=m: m[:, c * 128:(c + 1) * 128], 8, pT)
                        mT = mTr[i % 2]
                        P.op(act, lambda e, mT=mT: e.copy(mT[:], pT[:]), reads=[pT], writes=[mT])

                    def m3_y(i):
                        mT = mTr[i % 2]; py = pyr[i % 2]
                        for half in range(2):
                            for c in range(8):
                                P.op(pe, lambda e, c=c, half=half, py=py, mT=mT: e.matmul(py[:, half * 512:(half + 1) * 512], mT[:, c, :],
                                                                            wout[:, c, half * 512:(half + 1) * 512],
                                                                            start=(c == 0), stop=(c == 7)),
                                     reads=[mT, wout], writes=[py], inc=(c == 7 and half == 1))

                    def m3_n1(i):
                        py = pyr[i % 2]
                        sb = new_stat()
                        sumsq(py, py[:], sb, 0, 1024)
                        ry = rstd_of(sb, 0, 1024)
                        P.op(dve, lambda e, ry=ry, py=py: e.scalar_tensor_tensor(tt[:], py[:], ry, g_post, ALU.mult, ALU.mult),
                             reads=[py, sb, vecB], writes=[tt])

                    def m3_n2(i):
                        rows = slice(i * 128, (i + 1) * 128)
                        xt = xr[i % 3]; x1t = x1r[i % 2]
                        P.op(dve, lambda e, xt=xt, x1t=x1t: e.tensor_tensor(x1t[:], xt[:], tt[:], ALU.add), reads=[xt, tt], writes=[x1t])
                        P.dma(sp, X1[rows, :], x1t[:], x1t, DR)
                        sb = new_stat()
                        sumsq(x1t, x1t[:], sb, 0, 1024)
                        rx = rstd_of(sb, 0, 1024)
                        P.op(dve, lambda e, x1t=x1t, rx=rx: e.scalar_tensor_tensor(h2[:], x1t[:], rx, g_pre, ALU.mult, ALU.mult),
                             reads=[x1t, sb, vecB], writes=[h2])
                        transposes(h2, lambda c: h2[:, c * 128:(c + 1) * 128], 8, pT2)
                        P.op(act, lambda e, rows=rows: e.copy(H2T[:, :, rows], pT2[:]), reads=[pT2], writes=[H2T])

                    m3_load(0); m3_load(1)
                    m3_comb(0)
                    m3_y(0)
                    if NT > 1:
                        m3_comb(1)
                    for i in range(NT):
                        if i + 2 < NT:
                            m3_load(i + 2)
                        if i + 1 < NT:
                            m3_y(i + 1)
                        P.begin_chain(); m3_n1(i); m3_n2(i); cA = P.end_chain()
                        cB = []
                        if i + 2 < NT:
                            P.begin_chain(); m3_comb(i + 2); cB = P.end_chain()
                        P.interleave(cA, cB)
                P.barrier()
                if stop_after == "M3":
                    break
                with ExitStack() as s4:
                    P.stack = s4
                    NTS = S // 512
                    yTr = [P.sbuf("yT%d" % i, [128, 32, 512], BF16) for i in range(2)]
                    NWU = 3
                    wur = [P.sbuf("wu%d" % i, [128, 8, 256], BF16) for i in range(NWU)]
                    NWD = 4
                    wdr = [P.sbuf("wd%d" % i, [128, 1024], BF16) for i in range(NWD)]
                    cpk = P.sbuf("cpk", [128, 64, 4], F32)
                    halo_t = P.sbuf("halo", [128, 2, 64, 2], F32)
                    halo = [[P.view("halo%d_%d" % (a_, j), halo_t[:, a_, j, :]) for j in range(64)] for a_ in range(2)]
                    vecC = P.sbuf("vecC", [128, 1024], F32)
                    cr = [[P.sbuf("c%d_%d" % (i, j), [128, 512], F32) for j in range(3)] for i in range(3)]
                    x1r = [P.sbuf("fx1r%d" % i, [128, DM], F32) for i in range(2)]
                    x2r = [P.sbuf("fx2r%d" % i, [128, DM], F32) for i in range(2)]
                    ftt_ = P.sbuf("ftt", [128, DM], F32)
                    B = [P.psum("B%d" % i, [128, 512], F32) for i in range(8)]
                    P.dma(sp, cpk[:], cpk_in[l], DR, cpk)
                    P.dma(sp, vecC[:], vecC_in[l], DR, vecC)
                    P.op(dve, lambda e: e.memset(halo_t[:], 0.0), writes=halo[0] + halo[1])

                    events = []
                    for w_ in range(NTS + 1):
                        dsteps = []
                        if w_ >= 1:
                            for hbk in range(2):
                                dsteps += [("D", w_ - 1, hbk, j_) for j_ in range(32)] + [("E", w_ - 1, hbk, 0)]
                        if w_ < NTS:
                            nd = 0
                            for j_ in range(32):
                                events.append(("U", w_, 0, j_))
                                tgt = ((j_ + 1) * len(dsteps)) // 32
                                while nd < tgt:
                                    events.append(dsteps[nd]); nd += 1
                        else:
                            events += dsteps
                    loads = [("u", ev[3]) for ev in events if ev[0] == "U"]
                    loads = [("u", ev[3]) if ev[0] == "U" else ("d", ev[3]) for ev in events if ev[0] in ("U", "D")]
                    lst = {"li": 0, "u": 0, "d": 0}
                    slotq = {"u": [], "d": []}
                    PF = 2

                    def ensure(upto):
                        while lst["li"] <= min(upto, len(loads) - 1):
                            kind, j_ = loads[lst["li"]]
                            if kind == "u":
                                w_ = wur[lst["u"] % NWU]; lst["u"] += 1
                                P.dma(sp, w_[:], WUb[j_], DR, w_)
                            else:
                                w_ = wdr[lst["d"] % NWD]; lst["d"] += 1
                                P.dma(sp, w_[:], WDb[j_], DR, w_)
                            slotq[kind].append(w_)
                            lst["li"] += 1
                    ci = 0
                    uu = 0
                    pendg = None
                    xi = 0

                    def emit_U(s, j):
                        nonlocal_ = None
                        tok = slice(s * 512, (s + 1) * 512)
                        yT = yTr[s % 2]
                        wu = slotq["u"].pop(0)
                        pg = B[4 + (uu_[0] % 2) * 2]; pv = B[4 + (uu_[0] % 2) * 2 + 1]
                        gc, vc, gg = cr[uu_[0] % 3]
                        for (pp, off) in ((pg, 0), (pv, 128)):
                            for c in range(8):
                                P.op(pe, lambda e, pp=pp, off=off, c=c, wu=wu, tok=tok: e.matmul(pp[:], wu[:, c, off:off + 128], H2T[:, c, tok],
                                                                                       start=(c == 0), stop=(c == 7)),
                                     reads=[wu, H2T], writes=[pp], inc=(c == 7))
                        for (pp, cc, jj) in ((pg, gc, j), (pv, vc, 32 + j)):
                            w2 = cpk[:, jj, 2:3]; cb = cpk[:, jj, 3:4]
                            hn_ = halo[(s + 1) % 2][jj]
                            P.op(act, lambda e, pp=pp, cc=cc, w2=w2, cb=cb: e.activation(cc[:], pp[:], AF.Identity, scale=w2, bias=cb),
                                 reads=[pp, cpk], writes=[cc])
                            P.op(act, lambda e, hn_=hn_, pp=pp: e.copy(hn_[:, 0:2], pp[:, 510:512]), reads=[pp], writes=[hn_])
                        for (pp, cc, jj) in ((pg, gc, j), (pv, vc, 32 + j)):
                            w0 = cpk[:, jj, 0:1]; w1 = cpk[:, jj, 1:2]
                            hl_ = halo[s % 2][jj]
                            P.op(dve, lambda e, pp=pp, cc=cc, w1=w1: e.scalar_tensor_tensor(cc[:, 1:512], pp[:, 0:511], w1, cc[:, 1:512],
                                                                                           ALU.mult, ALU.add),
                                 reads=[pp, cpk, cc], writes=[cc])
                            P.op(dve, lambda e, hl_=hl_, cc=cc, w1=w1: e.scalar_tensor_tensor(cc[:, 0:1], hl_[:, 1:2], w1, cc[:, 0:1],
                                                                                             ALU.mult, ALU.add),
                                 reads=[hl_, cpk, cc], writes=[cc])
                            P.op(dve, lambda e, pp=pp, cc=cc, w0=w0: e.scalar_tensor_tensor(cc[:, 2:512], pp[:, 0:510], w0, cc[:, 2:512],
                                                                                           ALU.mult, ALU.add),
                                 reads=[pp, cpk, cc], writes=[cc])
                            P.op(dve, lambda e, hl_=hl_, cc=cc, w0=w0: e.scalar_tensor_tensor(cc[:, 0:2], hl_[:, 0:2], w0, cc[:, 0:2],
                                                                                             ALU.mult, ALU.add),
                                 reads=[hl_, cpk, cc], writes=[cc])

                        def fin(gc=gc, gg=gg, vc=vc, j=j, yT=yT):
                            P.op(act, lambda e: e.activation(gg[:], gc[:], AF.Gelu_apprx_tanh), reads=[gc], writes=[gg])
                            P.op(pool, lambda e: e.tensor_tensor(yT[:, j, :], gg[:], vc[:], ALU.mult), reads=[gg, vc], writes=[yT])
                        if pend_[0] is not None:
                            pend_[0]()
                        pend_[0] = fin
                        uu_[0] += 1
                        if j == 31:
                            pend_[0]()
                            pend_[0] = None

                    def emit_D(s, hb, j):
                        yT = yTr[s % 2]
                        wd = slotq["d"].pop(0)
                        for tl in range(2):
                            ts = hb * 2 + tl
                            for half in range(2):
                                bb = B[tl * 2 + half]
                                P.op(pe, lambda e, bb=bb, ts=ts, half=half, wd=wd, j=j, yT=yT: e.matmul(
                                    bb[:], yT[:, j, ts * 128:(ts + 1) * 128], wd[:, half * 512:(half + 1) * 512],
                                    start=(j == 0), stop=(j == 31)),
                                    reads=[yT, wd], writes=[bb], inc=(tl == 1 and half == 1))

                    def emit_E(s, hb):
                        for tl in range(2):
                            ts = hb * 2 + tl
                            rows = slice(s * 512 + ts * 128, s * 512 + (ts + 1) * 128)
                            x1t = x1r[tl]; x2t = x2r[tl]
                            P.dma(sp, x1t[:], X1[rows, :], DR, x1t)
                            sb = new_stat()
                            b0 = B[tl * 2]; b1 = B[tl * 2 + 1]
                            P.op(act, lambda e, b0=b0, sb=sb: e.activation(junk[:, 0:512], b0[:], AF.Square, accum_out=sb[:, 8:9]),
                                 reads=[b0], writes=[sb, junk])
                            P.op(act, lambda e, b1=b1, sb=sb: e.activation(junk[:, 512:1024], b1[:], AF.Square, accum_out=sb[:, 9:10]),
                                 reads=[b1], writes=[sb, junk])
                            P.op(dve, lambda e, sb=sb: e.tensor_tensor(sb[:, 0:1], sb[:, 8:9], sb[:, 9:10], ALU.add), reads=[sb], writes=[sb])
                            rf = rstd_of(sb, 0, 1024)
                            for half, bb in ((0, b0), (1, b1)):
                                P.op(dve, lambda e, half=half, bb=bb, rf=rf: e.scalar_tensor_tensor(
                                    ftt_[:, half * 512:(half + 1) * 512], bb[:], rf, vecC[:, half * 512:(half + 1) * 512], ALU.mult, ALU.mult),
                                    reads=[bb, sb, vecC], writes=[ftt_])
                            P.op(dve, lambda e, x1t=x1t, x2t=x2t: e.tensor_tensor(x2t[:], x1t[:], ftt_[:], ALU.add), reads=[x1t, ftt_], writes=[x2t])
                            P.dma(sp, Xdst[rows, :], x2t[:], x2t, DR)

                    uu_ = [0]
                    pend_ = [None]
                    for ev in events:
                        if ev[0] == "U":
                            ensure(ci + PF); ci += 1
                            emit_U(ev[1], ev[3])
                        elif ev[0] == "D":
                            ensure(ci + PF); ci += 1
                            emit_D(ev[1], ev[2], ev[3])
                        else:
                            emit_E(ev[1], ev[2])
                P.barrier()
            P.stack = st
        P.stack = st
        P.barrier()
        block = st.enter_context(nc.Block())
        P.emit(block)
    return nc


def _prep_inputs(inp):
    f = np.float32
    L = 2
    g = {k: np.asarray(v, dtype=f) for k, v in inp.items()}
    rep = lambda v: np.ascontiguousarray(np.broadcast_to(v[:, None, :], (L, 128, v.shape[-1])))
    shared = {}
    shared["w_in"] = np.ascontiguousarray(g["w_in"].reshape(L, 8, 128, 2560).transpose(0, 2, 1, 3))
    shared["w_out"] = np.ascontiguousarray(g["w_out"].reshape(L, 8, 128, 1024).transpose(0, 2, 1, 3))
    wu = g["w_up"].reshape(L, 8, 128, 2, 32, 128)
    shared["w_up"] = np.ascontiguousarray(wu.transpose(0, 4, 2, 1, 3, 5).reshape(L, 32, 128, 8, 256))
    shared["w_down"] = np.ascontiguousarray(g["w_down"].reshape(L, 32, 128, 1024))
    shared["wsT"] = np.ascontiguousarray(g["w_spatial"].transpose(0, 3, 1, 2))
    shared["bsp"] = np.ascontiguousarray(g["b_spatial"].transpose(0, 2, 1))
    shared["vecA"] = rep(np.concatenate([g["pre_mix_norm"], g["v_norm_g"], g["v_norm_b"], g["out_norm_a"]], -1))
    shared["vecB"] = rep(np.concatenate([g["out_norm_b"], g["post_mix_norm"], g["pre_ffn_norm"]], -1))
    shared["vecC"] = rep(g["post_ffn_norm"])
    cp = np.concatenate([g["conv_w"], g["conv_b"][:, None, :]], 1)
    shared["cpk"] = np.ascontiguousarray(cp.reshape(L, 4, 64, 128).transpose(0, 3, 2, 1))
    t = np.arange(S, dtype=f)
    inv = (f(500000.0) ** (-np.arange(0, 16, 2, dtype=f) / f(16))).astype(f)
    ang = (t[:, None] * inv[None, :]).astype(f)
    cs = np.stack([np.cos(ang), np.sin(ang)], 0).astype(f)
    shared["rope"] = np.ascontiguousarray(cs.reshape(2, NT, 128, 8).transpose(2, 0, 1, 3))
    k = np.arange(128)[:, None]; q = np.arange(128)[None, :]
    cur = (k <= q).astype(f); prev = (k >= q).astype(f)
    shared["cst"] = np.ascontiguousarray(np.concatenate([np.eye(128, dtype=f), cur, cur, prev, cur, prev], 1))
    return g["x"], shared


_NC_CACHE = {}


def kernel(**inputs):
    x, shared = _prep_inputs(inputs)
    if "nc" not in _NC_CACHE:
        _NC_CACHE["nc"] = build_program()
    nc = _NC_CACHE["nc"]
    in_maps = []
    for c in range(NCORES):
        m = dict(shared)
        m["x"] = np.ascontiguousarray(x[c])
        in_maps.append(m)
    res = run_bass_kernel_spmd(nc, in_maps, core_ids=list(range(NCORES)))
    return np.stack([np.asarray(r["out"], dtype=np.float32) for r in res.results], 0)
```
